# Optimizing a Trainium2 kernel written in Bass

```python
import math
import jax, jax.numpy as jnp
from jax import lax
import numpy as np

D_MODEL = 4096
BATCH = 4
SEQ = 4096
DEPTH = 1

MIX_WIDTH = D_MODEL
REC_WIDTH = MIX_WIDTH // 2
ATTN_WIDTH = MIX_WIDTH - REC_WIDTH
HEAD_DIM = 128
N_ATTN_HEADS = ATTN_WIDTH // HEAD_DIM
N_REC_HEADS = 16
REC_HEAD_DIM = REC_WIDTH // N_REC_HEADS
CONV_WIDTH = 4
LRU_C = 8.0
DILATED_PATTERNS = ((128, 1), (512, 4), (2048, 16))
BLOCK = 128
REL_BUCKETS = 32
REL_MAX_DISTANCE = 2048
D_FF = ((8 * D_MODEL + 3 * 256 - 1) // (3 * 256)) * 256
N_ADA = 6
EPS = 1e-6
NEG_INF = -1e30

kernel_name = "hybrid_rglru_dilated_attn_block"


def rms_norm(x, g):
    x32 = x.astype(jnp.float32)
    y = x32 * lax.rsqrt(jnp.mean(x32 * x32, axis=-1, keepdims=True) + EPS) * g.astype(jnp.float32)
    return y.astype(x.dtype)


def t5_bucket(n):
    max_exact = REL_BUCKETS // 2
    nf = jnp.maximum(n, 1).astype(jnp.float32)
    large = max_exact + (jnp.log(nf / max_exact) / math.log(REL_MAX_DISTANCE / max_exact)
                         * (REL_BUCKETS - max_exact)).astype(jnp.int32)
    large = jnp.minimum(large, REL_BUCKETS - 1)
    return jnp.where(n < max_exact, n, large)


def causal_conv(x, w, b):
    C = x.shape[-1]
    y = lax.conv_general_dilated(x, w[:, None, :].astype(x.dtype), window_strides=(1,),
                                 padding=((CONV_WIDTH - 1, 0),),
                                 dimension_numbers=('NWC', 'WIO', 'NWC'),
                                 feature_group_count=C)
    return y + b.astype(x.dtype)


def rg_lru(xr, w_a, b_a, w_i, b_i, lam):
    B, S, R = xr.shape
    x32 = xr.astype(jnp.float32)
    xh = x32.reshape(B, S, N_REC_HEADS, REC_HEAD_DIM)
    r = jax.nn.sigmoid(jnp.einsum('bshi,hij->bshj', xh, w_a.astype(jnp.float32)).reshape(B, S, R)
                       + b_a.astype(jnp.float32))
    i = jax.nn.sigmoid(jnp.einsum('bshi,hij->bshj', xh, w_i.astype(jnp.float32)).reshape(B, S, R)
                       + b_i.astype(jnp.float32))
    log_a = -LRU_C * r * jax.nn.softplus(-lam.astype(jnp.float32))
    a = jnp.exp(log_a)
    u = jnp.sqrt(-jnp.expm1(2.0 * log_a)) * (i * x32)

    def combine(left, right):
        a1, b1 = left
        a2, b2 = right
        return a1 * a2, a2 * b1 + b2

    _, h = lax.associative_scan(combine, (a, u), axis=1)
    return h


def dilated_branch(q, k, v, rel_bias, window, dil):
    B, S, H, Dh = q.shape
    span = BLOCK * dil
    S_pad = -(-S // span) * span
    pad = S_pad - S
    L = S_pad // dil
    nb = L // BLOCK

    def split(t):
        t = jnp.pad(t, ((0, 0), (0, pad), (0, 0), (0, 0)))
        t = t.reshape(B, L, dil, H, Dh).transpose(0, 2, 1, 3, 4)
        return t.reshape(B, dil, nb, BLOCK, H, Dh)

    def with_prev(t):
        prev = jnp.concatenate([jnp.zeros_like(t[:, :, :1]), t[:, :, :-1]], axis=2)
        return jnp.concatenate([prev, t], axis=3)

    qb = split(q)
    kc = with_prev(split(k))
    vc = with_prev(split(v))

    s = jnp.einsum('brnqhd,brnkhd->brnhqk', qb, kc) * (HEAD_DIM ** -0.5)
    qi = jnp.arange(BLOCK, dtype=jnp.int32)[:, None]
    kj = jnp.arange(2 * BLOCK, dtype=jnp.int32)[None, :]
    dist = qi + BLOCK - kj
    band = (dist >= 0) & (dist <= window // dil)
    blk = jnp.arange(nb, dtype=jnp.int32)[:, None, None]
    valid = band[None] & ((blk > 0) | (kj[None] >= BLOCK))
    bias = rel_bias.astype(jnp.float32)[t5_bucket(jnp.maximum(dist, 0) * dil)]
    s = s + bias.transpose(2, 0, 1)[None, None, None]
    s = jnp.where(valid[None, None, :, None], s, NEG_INF)

    m = jnp.max(s, axis=-1, keepdims=True)
    e = jnp.exp(s - m)
    den = jnp.sum(e, axis=-1)
    o = jnp.einsum('brnhqk,brnkhd->brnqhd', e, vc) / den.transpose(0, 1, 2, 4, 3)[..., None]
    lse = (m[..., 0] + jnp.log(den)).transpose(0, 1, 2, 4, 3)

    o = o.reshape(B, dil, L, H, Dh).transpose(0, 2, 1, 3, 4).reshape(B, S_pad, H, Dh)[:, :S]
    lse = lse.reshape(B, dil, L, H).transpose(0, 2, 1, 3).reshape(B, S_pad, H)[:, :S]
    return o, lse


def dilated_attention(q, k, v, rel_bias):
    q, k, v = (t.astype(jnp.float32) for t in (q, k, v))
    outs, lses = [], []
    for window, dil in DILATED_PATTERNS:
        o, lse = dilated_branch(q, k, v, rel_bias, window, dil)
        outs.append(o)
        lses.append(lse)
    w = jax.nn.softmax(jnp.stack(lses, axis=0), axis=0)
    return jnp.sum(w[..., None] * jnp.stack(outs, axis=0), axis=0)


def swiglu(h, w_gate, w_up, w_down):
    return (jax.nn.silu(h @ w_gate) * (h @ w_up)) @ w_down


def setup_inputs(seed: int = 0) -> dict:
    key = jax.random.key(seed)
    ks = jax.random.split(key, 24)
    f32 = jnp.float32

    def nrm(k, shape, scale):
        return jax.random.normal(k, shape, f32) * scale

    a8 = jax.random.uniform(ks[13], (DEPTH, REC_WIDTH), f32, 0.9, 0.999)
    a = a8 ** (1.0 / LRU_C)
    lru_lambda = jnp.log(a) - jnp.log1p(-a)
    return {
        "x": nrm(ks[0], (BATCH, SEQ, D_MODEL), 1.0),
        "c": nrm(ks[1], (BATCH, D_MODEL), 1.0),
        "ada_w": nrm(ks[2], (DEPTH, D_MODEL, N_ADA * D_MODEL), 0.5 * D_MODEL ** -0.5),
        "ada_b": nrm(ks[3], (DEPTH, N_ADA * D_MODEL), 0.01),
        "norm1_g": 1.0 + nrm(ks[4], (DEPTH, D_MODEL), 0.02),
        "norm2_g": 1.0 + nrm(ks[5], (DEPTH, D_MODEL), 0.02),
        "w_in": nrm(ks[6], (DEPTH, D_MODEL, 2 * REC_WIDTH + 3 * ATTN_WIDTH), D_MODEL ** -0.5),
        "conv_w": nrm(ks[7], (DEPTH, CONV_WIDTH, REC_WIDTH), CONV_WIDTH ** -0.5),
        "conv_b": nrm(ks[8], (DEPTH, REC_WIDTH), 0.01),
        "rg_w_a": nrm(ks[9], (DEPTH, N_REC_HEADS, REC_HEAD_DIM, REC_HEAD_DIM), REC_HEAD_DIM ** -0.5),
        "rg_b_a": nrm(ks[10], (DEPTH, REC_WIDTH), 0.01),
        "rg_w_i": nrm(ks[11], (DEPTH, N_REC_HEADS, REC_HEAD_DIM, REC_HEAD_DIM), REC_HEAD_DIM ** -0.5),
        "rg_b_i": nrm(ks[12], (DEPTH, REC_WIDTH), 0.01),
        "lru_lambda": lru_lambda,
        "rel_bias": nrm(ks[14], (REL_BUCKETS, N_ATTN_HEADS), 0.5),
        "gnorm_rec": 1.0 + nrm(ks[15], (DEPTH, REC_WIDTH), 0.02),
        "gnorm_attn": 1.0 + nrm(ks[16], (DEPTH, ATTN_WIDTH), 0.02),
        "w_out": nrm(ks[17], (DEPTH, MIX_WIDTH, D_MODEL), MIX_WIDTH ** -0.5),
        "w_gate": nrm(ks[18], (DEPTH, D_MODEL, D_FF), D_MODEL ** -0.5),
        "w_up": nrm(ks[19], (DEPTH, D_MODEL, D_FF), D_MODEL ** -0.5),
        "w_down": nrm(ks[20], (DEPTH, D_FF, D_MODEL), D_FF ** -0.5),
        "final_g": 1.0 + nrm(ks[21], (D_MODEL,), 0.02),
    }


def reference(x, c, ada_w, ada_b, norm1_g, norm2_g, w_in, conv_w, conv_b, rg_w_a, rg_b_a,
              rg_w_i, rg_b_i, lru_lambda, rel_bias, gnorm_rec, gnorm_attn, w_out,
              w_gate, w_up, w_down, final_g):
    B, S, D = x.shape
    cond = jax.nn.silu(c)
    col_splits = [REC_WIDTH, 2 * REC_WIDTH, 2 * REC_WIDTH + ATTN_WIDTH, 2 * REC_WIDTH + 2 * ATTN_WIDTH]
    for l in range(DEPTH):
        mod = cond @ ada_w[l] + ada_b[l]
        sh1, sc1, g1, sh2, sc2, g2 = jnp.split(mod, N_ADA, axis=-1)

        h = rms_norm(x, norm1_g[l]) * (1.0 + sc1[:, None]) + sh1[:, None]
        proj = h @ w_in[l]
        xr, yg, q, k, v = jnp.split(proj, col_splits, axis=-1)
        xr = causal_conv(xr, conv_w[l], conv_b[l])
        rec = rg_lru(xr, rg_w_a[l], rg_b_a[l], rg_w_i[l], rg_b_i[l], lru_lambda[l])
        rec = rec * jax.nn.gelu(yg.astype(jnp.float32))
        att = dilated_attention(q.reshape(B, S, N_ATTN_HEADS, HEAD_DIM),
                                k.reshape(B, S, N_ATTN_HEADS, HEAD_DIM),
                                v.reshape(B, S, N_ATTN_HEADS, HEAD_DIM),
                                rel_bias).reshape(B, S, ATTN_WIDTH)
        mixed = jnp.concatenate([rms_norm(rec.astype(x.dtype), gnorm_rec[l]),
                                 rms_norm(att.astype(x.dtype), gnorm_attn[l])], axis=-1)
        x = x + g1[:, None] * (mixed @ w_out[l])

        h = rms_norm(x, norm2_g[l]) * (1.0 + sc2[:, None]) + sh2[:, None]
        x = x + g2[:, None] * swiglu(h, w_gate[l], w_up[l], w_down[l])
    return rms_norm(x, final_g)
```

```python
import math
from contextlib import ExitStack

import numpy as np
import concourse.bass as bass
import concourse.mybir as mybir
from concourse.bass_utils import run_bass_kernel_spmd

F32 = mybir.dt.float32
BF16 = mybir.dt.bfloat16
AF = mybir.ActivationFunctionType
ALU = mybir.AluOpType
AX = mybir.AxisListType

D = 4096
KC = 32
T = 2048
TH = 2048
NCOL = 10240
DFF = 11008
NFC = DFF // 128
EPS = 1e-6
NEG = -1e30
N_ADA_B = 12
QSCALE = 128 ** -0.5

R_N1G, R_N2G, R_FG, R_CW, R_CB, R_BA, R_BI, R_LAM, R_GR, R_GA, R_ADAB, R_TOT = (
    0, 32, 64, 96, 160, 176, 192, 208, 224, 240, 256, 448)


class Tile:
    __slots__ = ("w", "readers", "excl")

    def __init__(self, excl=False):
        self.w = None
        self.readers = {}
        self.excl = excl


class Eng:
    def __init__(self, name, handle, sem):
        self.name = name
        self.handle = handle
        self.sem = sem
        self.count = 0
        self.waited = {}
        self.same_wait = name != "tensor"
        self.dsems = []
        self.dvals = []
        self.dnext = 0


class FW:
    def __init__(self, nc, stack, n_dma=14):
        self.nc = nc
        self.engs = {}
        for name in ("tensor", "vector", "scalar", "gpsimd", "sync"):
            sem = stack.enter_context(nc.semaphore("s_" + name))
            self.engs[name] = Eng(name, getattr(nc, name), sem)
        for qn in ("sync", "gpsimd"):
            e = self.engs[qn]
            for i in range(n_dma):
                e.dsems.append(stack.enter_context(nc.semaphore("d_%s%d" % (qn, i))))
                e.dvals.append(0)
        self.n_ops = 0

    def _collect(self, eng, reads, writes, after=()):
        deps = []
        for t in reads:
            if t.w is not None:
                deps.append(t.w)
        for t in list(writes) + list(after):
            if t.w is not None:
                deps.append(t.w)
            deps.extend(t.readers.values())
        waits = {}
        for (sem, val, owner) in deps:
            if owner is eng and not eng.same_wait:
                continue
            k = id(sem)
            if eng.waited.get(k, -1) >= val:
                continue
            if k not in waits or waits[k][1] < val:
                waits[k] = (sem, val)
        for k, (sem, val) in waits.items():
            eng.waited[k] = val
        return list(waits.values())

    def _mark(self, ev, reads, writes):
        k = id(ev[0])
        for t in reads:
            old = t.readers.get(k)
            if old is None or old[1] < ev[1]:
                t.readers[k] = ev
        for t in writes:
            t.w = ev
            t.readers = {}
        self.n_ops += 1

    def op(self, engname, fn, reads=(), writes=(), inc=True, after=()):
        eng = self.engs[engname]
        ex = [t for t in reads if t.excl]
        if ex:
            reads = [t for t in reads if not t.excl]
            writes = list(writes) + ex
        waits = self._collect(eng, reads, writes, after)
        h = eng.handle
        for (s, v) in waits:
            h.wait_ge(s, v)
        if inc:
            eng.count += 1
            ev = (eng.sem, eng.count, eng)
            fn(h).then_inc(eng.sem, 1)
        else:
            ev = (eng.sem, eng.count + 1, eng)
            fn(h)
        self._mark(ev, reads, writes)
        return ev

    def dma(self, qname, fn, reads=(), writes=(), after=()):
        eng = self.engs[qname]
        waits = self._collect(eng, reads, writes, after)
        i = eng.dnext
        eng.dnext = (i + 1) % len(eng.dsems)
        dsem = eng.dsems[i]
        prev = eng.dvals[i]
        if prev > 0 and eng.waited.get(id(dsem), -1) < prev:
            waits.append((dsem, prev))
            eng.waited[id(dsem)] = prev
        eng.dvals[i] = prev + 16
        ev = (dsem, prev + 16, None)
        h = eng.handle
        for (s, v) in waits:
            h.wait_ge(s, v)
        fn(h).then_inc(dsem, 16)
        self._mark(ev, reads, writes)
        return ev

    def barrier(self):
        targets = []
        for e in self.engs.values():
            if e.count > 0:
                targets.append((e.sem, e.count, e))
            for i, ds in enumerate(e.dsems):
                if e.dvals[i] > 0:
                    targets.append((ds, e.dvals[i], None))
        for e in self.engs.values():
            for (sem, val, owner) in targets:
                if owner is e:
                    continue
                if e.waited.get(id(sem), -1) >= val:
                    continue
                e.waited[id(sem)] = val
                e.handle.wait_ge(sem, val)

    def wait_all(self, engname, tiles):
        eng = self.engs[engname]
        waits = self._collect(eng, tiles, tiles)
        for (s, v) in waits:
            eng.handle.wait_ge(s, v)


class Ring:
    def __init__(self, aps, excl=False):
        self.items = [(a, Tile(excl)) for a in aps]
        self.i = 0

    def next(self):
        r = self.items[self.i]
        self.i = (self.i + 1) % len(self.items)
        return r


def _t5_bucket(n):
    nf = np.maximum(n, 1).astype(np.float32)
    large = 16 + (np.log(nf / np.float32(16)) / np.float32(math.log(2048 / 16)) * np.float32(16)).astype(np.int32)
    large = np.minimum(large, 31)
    return np.where(n < 16, n, large)


def _bias_tables(rel_bias):
    k = np.arange(128)[:, None, None]
    j = np.arange(2)[None, :, None]
    q = np.arange(128)[None, None, :]
    dist = q + 128 - (j * 128 + k)
    valid = (dist >= 0) & (dist <= 128)
    out = np.empty((3, 16, 128, 256), np.float32)
    for p, dil in enumerate((1, 4, 16)):
        b = _t5_bucket((np.maximum(dist, 0) * dil).astype(np.int32))
        g = rel_bias[b]
        g = np.where(valid[..., None], g, np.float32(NEG)).astype(np.float32)
        out[p] = np.transpose(g, (3, 0, 1, 2)).reshape(16, 128, 256)
    return out


def _prep_inputs(inp):
    x = np.ascontiguousarray(inp["x"])
    pv = np.concatenate([
        inp["norm1_g"][0].reshape(32, 128), inp["norm2_g"][0].reshape(32, 128),
        inp["final_g"].reshape(32, 128), inp["conv_w"][0].reshape(64, 128),
        inp["conv_b"][0].reshape(16, 128), inp["rg_b_a"][0].reshape(16, 128),
        inp["rg_b_i"][0].reshape(16, 128), inp["lru_lambda"][0].reshape(16, 128),
        inp["gnorm_rec"][0].reshape(16, 128), inp["gnorm_attn"][0].reshape(16, 128),
        inp["ada_b"][0].reshape(192, 128)], axis=0).astype(np.float32)
    assert pv.shape == (R_TOT, 128)
    bt = _bias_tables(inp["rel_bias"])
    bte0 = bt.copy()
    bte0[:, :, :, 0:128] = np.float32(NEG)
    shared = {
        "pvec": pv, "ada_w": inp["ada_w"][0], "w_in": inp["w_in"][0],
        "rg_w_a": inp["rg_w_a"][0], "rg_w_i": inp["rg_w_i"][0], "w_out": inp["w_out"][0],
        "w_gate": inp["w_gate"][0], "w_up": inp["w_up"][0], "w_down": inp["w_down"][0],
        "btab": bt,
    }
    zeros_h = np.zeros((TH, D), np.float32)
    maps = []
    for b in range(4):
        for s in range(2):
            m = dict(shared)
            m["xm"] = x[b, s * T:(s + 1) * T]
            m["xh"] = x[b, 0:TH] if s == 1 else zeros_h
            m["cvec"] = np.ascontiguousarray(inp["c"][b].reshape(128, 32))
            m["flag"] = np.full((128, 2), float(s), np.float32)
            m["btabe"] = bt if s == 1 else bte0
            maps.append(m)
    return maps


def build_program(debug=False, stop_after=None, ffn_passes=2):
    nc = bass.Bass("TRN2", target_bir_lowering=False)

    def din(name, shape, dt=F32):
        return nc.dram_tensor(name, list(shape), dt, kind="ExternalInput").ap()

    def dscr(name, shape, dt):
        kind = "ExternalOutput" if debug else "Internal"
        return nc.dram_tensor(name, list(shape), dt, kind=kind).ap()

    xm = din("xm", [T, D]); xh = din("xh", [TH, D]); cvec = din("cvec", [128, 32])
    flag_d = din("flag", [128, 2]); pvec = din("pvec", [R_TOT, 128])
    ada_w = din("ada_w", [D, 6 * D]); w_in = din("w_in", [D, NCOL])
    rg_w_a = din("rg_w_a", [16, 128, 128]); rg_w_i = din("rg_w_i", [16, 128, 128])
    w_out = din("w_out", [D, D]); w_gate = din("w_gate", [D, DFF]); w_up = din("w_up", [D, DFF])
    w_down = din("w_down", [DFF, D]); btab = din("btab", [3, 16, 128, 256]); btabe = din("btabe", [3, 16, 128, 256])
    y_out = nc.dram_tensor("y", [T, D], F32, kind="ExternalOutput").ap()

    xT_s = dscr("xT_s", [KC, 128, T], F32)
    q_s = dscr("q_s", [16, 128, T], BF16)
    k_s = dscr("k_s", [16, 128, TH + T], BF16)
    v_s = dscr("v_s", [16, 128, TH + T], BF16)
    mix_s = dscr("mix_s", [KC, 128, T], F32)
    mod_dbg = nc.dram_tensor("mod_dbg", [128, 192], F32, kind="ExternalOutput").ap() if debug else None

    with ExitStack() as st:
        fw = FW(nc, st)
        sb = lambda name, shape, dt=F32, stack=st: stack.enter_context(nc.sbuf_tensor(name, list(shape), dt))
        pp = lambda name, shape, dt=F32, stack=st: stack.enter_context(nc.psum_tensor(name, list(shape), dt))

        xT_t = [[Tile() for _ in range(4)] for _ in range(KC)]
        q_t = [[Tile() for _ in range(4)] for _ in range(16)]
        k_t = [[Tile() for _ in range(8)] for _ in range(16)]
        v_t = [[Tile() for _ in range(8)] for _ in range(16)]
        mix_t = [[Tile() for _ in range(4)] for _ in range(KC)]

        ident = sb("ident", [128, 128]); identb = sb("identb", [128, 128], BF16)
        onesb = sb("onesb", [128, 128], BF16)
        t_const = Tile()
        fw.op("vector", lambda h: h.memset(ident[:], 0.0), writes=[t_const])
        fw.op("gpsimd", lambda h: h.affine_select(out=ident[:], in_=ident[:], pattern=[[-1, 128]],
                                                   compare_op=ALU.not_equal, fill=1.0, base=0, channel_multiplier=1),
              reads=[t_const], writes=[t_const])
        fw.op("vector", lambda h: h.tensor_copy(out=identb[:], in_=ident[:]), reads=[t_const], writes=[t_const])
        fw.op("vector", lambda h: h.memset(onesb[:], 1.0), writes=[t_const])

        P_sb = sb("P_sb", [128, R_TOT]); modT = sb("modT", [128, 192])
        W1 = sb("W1", [128, 32]); W2 = sb("W2", [128, 32])
        flag = sb("flag_sb", [128, 2])
        m8sp = sb("m8sp", [128, 16]); m16sp = sb("m16sp", [128, 16])
        t_par = Tile(); t_mod = Tile(); t_mod2 = Tile()
        cs = sb("cs", [128, 32], F32)
        fw.dma("sync", lambda h: h.dma_start(out=flag[:], in_=flag_d), writes=[t_par])

        with ExitStack() as ph:
            ptmp = sb("ptmp", [112, 4, 128], F32, ph)
            crep = sb("crep", [128, 32, 128], BF16, ph)
            wa_ring = Ring([sb("wa%d" % i, [128, 32, 512], BF16, ph) for i in range(2)])
            ps0 = pp("ps0", [128, 512], F32, ph)
            psm = Ring([pp("psm%d" % i, [128, 512], F32, ph) for i in range(2)], excl=True)
            dtmp = Ring([sb("dtmp%d" % i, [128, 4, 128], F32, ph) for i in range(2)])
            t_ptmp = Tile(); t_ps0 = Tile(True); t_cs = Tile()
            fw.dma("sync", lambda h: h.dma_start(out=ptmp[:], in_=pvec.rearrange("(a r) c -> r a c", r=112)), writes=[t_ptmp])
            for a in range(4):
                fw.op("tensor", lambda h, a=a: h.transpose(ps0[:, a * 112:(a + 1) * 112], ptmp[:, a, :], ident[0:112, 0:112]),
                      reads=[t_ptmp, t_const], writes=[t_ps0])
            fw.op("vector", lambda h: h.tensor_copy(out=P_sb[:], in_=ps0[:, 0:R_TOT]), reads=[t_ps0], writes=[t_par])
            fw.dma("sync", lambda h: h.dma_start(out=cs[:], in_=cvec), writes=[t_cs])
            fw.op("scalar", lambda h: h.activation(out=cs[:], in_=cs[:], func=AF.Silu), reads=[t_cs], writes=[t_cs])
            fw.op("vector", lambda h: h.tensor_copy(out=crep[:], in_=cs[:].unsqueeze(2).to_broadcast([128, 32, 128])),
                  reads=[t_cs], writes=[t_cs])
            lam = P_sb[:, R_LAM:R_LAM + 16]
            e_ = sb("sp_e", [128, 16], F32, ph); z_ = sb("sp_z", [128, 16], F32, ph)
            z2 = sb("sp_z2", [128, 16], F32, ph); pz = sb("sp_pz", [128, 16], F32, ph)
            t_sp = Tile()
            fw.op("scalar", lambda h: h.activation(out=e_[:], in_=lam, func=AF.Abs), reads=[t_par], writes=[t_sp])
            fw.op("scalar", lambda h: h.activation(out=e_[:], in_=e_[:], func=AF.Exp, scale=-1.0), reads=[t_sp], writes=[t_sp])
            V = lambda fn, r=(t_sp, t_par), w=(t_sp,): fw.op("vector", fn, reads=list(r), writes=list(w))
            V(lambda h: h.tensor_scalar(out=z_[:], in0=e_[:], scalar1=2.0, scalar2=None, op0=ALU.add))
            V(lambda h: h.reciprocal(out=z_[:], in_=z_[:]))
            V(lambda h: h.tensor_tensor(out=z_[:], in0=z_[:], in1=e_[:], op=ALU.mult))
            V(lambda h: h.tensor_tensor(out=z2[:], in0=z_[:], in1=z_[:], op=ALU.mult))
            V(lambda h: h.tensor_scalar(out=pz[:], in0=z2[:], scalar1=1.0 / 11.0, scalar2=1.0 / 9.0, op0=ALU.mult, op1=ALU.add))
            for cst in (1.0 / 7.0, 1.0 / 5.0, 1.0 / 3.0, 1.0):
                V(lambda h: h.tensor_tensor(out=pz[:], in0=pz[:], in1=z2[:], op=ALU.mult))
                V(lambda h, cst=cst: h.tensor_scalar(out=pz[:], in0=pz[:], scalar1=cst, scalar2=None, op0=ALU.add))
            V(lambda h: h.tensor_tensor(out=pz[:], in0=pz[:], in1=z_[:], op=ALU.mult))
            V(lambda h: h.tensor_scalar(out=z_[:], in0=lam, scalar1=-1.0, scalar2=0.0, op0=ALU.mult, op1=ALU.max))
            V(lambda h: h.scalar_tensor_tensor(out=pz[:], in0=pz[:], scalar=2.0, in1=z_[:], op0=ALU.mult, op1=ALU.add))
            V(lambda h: h.tensor_scalar(out=m8sp[:], in0=pz[:], scalar1=-8.0, scalar2=None, op0=ALU.mult), w=(t_sp, t_par))
            V(lambda h: h.tensor_scalar(out=m16sp[:], in0=pz[:], scalar1=-16.0, scalar2=None, op0=ALU.mult), w=(t_sp, t_par))

            def adaln_chunk(j, crep, t_crep, wa_ring, ps, t_ps, dtmp, t_out):
                wa, t_wa = wa_ring.next()
                for hh in range(2):
                    fw.dma("gpsimd", lambda h, wa=wa, j=j, hh=hh: h.dma_start(
                        out=wa[:, hh * 16:(hh + 1) * 16, :],
                        in_=ada_w[:, j * 512:(j + 1) * 512].rearrange("(p c) n -> p c n", c=32)[:, hh * 16:(hh + 1) * 16, :]),
                        writes=[t_wa])
                for c in range(32):
                    fw.op("tensor", lambda h, ps=ps, wa=wa, c=c: h.matmul(ps[:], lhsT=crep[:, c, :], rhs=wa[:, c, :],
                                                                          start=(c == 0), stop=(c == 31)),
                          reads=[t_crep, t_wa], writes=[t_ps], inc=(c == 31))
                dt_, t_dt = dtmp.next()
                fw.op("vector", lambda h, ps=ps, dt_=dt_: h.tensor_tensor(
                    out=dt_[:], in0=ps[:].rearrange("p (a m) -> p a m", a=4),
                    in1=ident[:].unsqueeze(1).to_broadcast([128, 4, 128]), op=ALU.mult),
                    reads=[t_ps, t_const], writes=[t_dt])
                fw.op("vector", lambda h, dt_=dt_, j=j: h.tensor_reduce(out=modT[:, 4 * j:4 * j + 4], in_=dt_[:], axis=AX.X, op=ALU.add),
                      reads=[t_dt], writes=[t_out])

            for j in range(16):
                ps, t_ps = psm.next()
                adaln_chunk(j, crep, t_cs, wa_ring, ps, t_ps, dtmp, t_mod)
            fw.op("vector", lambda h: h.tensor_tensor(out=modT[:, 0:64], in0=modT[:, 0:64], in1=P_sb[:, R_ADAB:R_ADAB + 64], op=ALU.add),
                  reads=[t_mod, t_par], writes=[t_mod])
            fw.op("vector", lambda h: h.scalar_tensor_tensor(out=W1[:], in0=modT[:, 32:64], scalar=1.0, in1=P_sb[:, R_N1G:R_N1G + 32],
                                                             op0=ALU.add, op1=ALU.mult), reads=[t_mod, t_par], writes=[t_mod])
            fw.barrier()
        t_md = Tile()
        sh1 = modT[:, 0:32]; g1 = modT[:, 64:96]; sh2 = modT[:, 96:128]; g2 = modT[:, 160:192]
        tP = [t_par, t_mod]
        tP2 = [t_par, t_mod, t_mod2]

        if stop_after == 0:
            return _finish(nc, fw, y_out, st, debug, [])

        with ExitStack() as ph:
            hT = sb("hT", [128, KC, 1024], BF16, ph)
            t_hT = [Tile() for _ in range(8)]
            XT = [sb("xt%d" % i, [128, D], F32, ph) for i in range(2)]
            t_XT = [Tile(), Tile()]
            xc = XT[0][:, 0:1024]; r_sb = XT[0][:, 1024:2048]; i_sb = XT[0][:, 2048:3072]; a_sb = XT[0][:, 3072:4096]
            q_sb = XT[1][:, 0:1024]; hs_bufs = [XT[1][:, 1024:2048], XT[1][:, 2048:3072]]
            xcb = sb("xcb", [128, 1024], BF16, ph)
            t_rec = Tile(); t_hsb = [Tile(), Tile()]
            alias0 = [t_rec]; alias1 = [t_rec] + t_hsb
            junk = sb("junk", [128, 512], BF16, ph); t_junk = Tile()
            st_ring = Ring([sb("stat%d" % i, [128, 16], F32, ph) for i in range(2)])
            crepB = sb("crepB", [128, 32, 128], BF16, ph); t_crepB = Tile()
            fw.op("vector", lambda h: h.tensor_copy(out=crepB[:], in_=cs[:].unsqueeze(2).to_broadcast([128, 32, 128])),
                  reads=[t_cs], writes=[t_crepB])
            rhs2_ring = Ring([sb("rhs2_%d" % i, [128, 256], F32, ph) for i in range(2)])
            for (r2, t_r2) in rhs2_ring.items:
                fw.op("vector", lambda h, r2=r2: h.tensor_copy(out=r2[:, 0:128], in_=ident[:]), reads=[t_const], writes=[t_r2])
            xTst_ring = Ring([sb("xTst%d" % i, [128, 8, 128], F32, ph) for i in range(2)])
            psA = Ring([pp("psA%d" % i, [128, 512], F32, ph) for i in range(2)], excl=True)
            psB = Ring([pp("psB%d" % i, [128, 512], F32, ph) for i in range(4)], excl=True)
            psG = Ring([pp("psG%d" % i, [128, 512], F32, ph) for i in range(2)], excl=True)
            psAx = Ring([])
            psAx.items = psA.items + psB.items
            wt_ring = Ring([sb("wt%d" % i, [128, KC, 256], BF16, ph) for i in range(3)])
            ev_ring = Ring([sb("evb%d" % i, [128, 512], BF16, ph) for i in range(3)])
            wab = sb("wab", [128, 16, 128], BF16, ph); wib = sb("wib", [128, 16, 128], BF16, ph); t_wg = Tile()
            fw.dma("gpsimd", lambda h: h.dma_start(out=wab[:], in_=rg_w_a.rearrange("h i j -> i h j")), writes=[t_wg])
            fw.dma("gpsimd", lambda h: h.dma_start(out=wib[:], in_=rg_w_i.rearrange("h i j -> i h j")), writes=[t_wg])
            xr_ring = Ring([sb("xr%d" % i, [128, 1028], F32, ph) for i in range(4)])
            gl_ring = Ring([sb("gl%d" % i, [128, 1024], F32, ph) for i in range(2)])
            hist = sb("hist", [128, 16, 3], F32, ph); state = sb("state", [128, 16], F32, ph); t_hs = Tile()
            fw.op("vector", lambda h: h.memset(hist[:], 0.0), writes=[t_hs])
            fw.op("vector", lambda h: h.memset(state[:], 0.0), writes=[t_hs])
            cw = lambda tap, hd: P_sb[:, R_CW + tap * 16 + hd:R_CW + tap * 16 + hd + 1]
            hs_i = [0]

            def phase_a(src, tok0, halo):
                def load_x(tt):
                    r0_ = tok0 + tt * 128
                    fw.dma("sync", lambda h: h.dma_start(out=XT[tt % 2][:], in_=src[r0_:r0_ + 128, :]), writes=[t_XT[tt % 2]],
                           after=(alias0 if tt % 2 == 0 else alias1))

                load_x(0)
                for tt in range(8):
                    xt = XT[tt % 2]; t_xt = t_XT[tt % 2]
                    r0 = tok0 + tt * 128
                    if tt + 1 < 8:
                        load_x(tt + 1)
                    stt, t_st = st_ring.next()
                    for hf in range(8):
                        fw.op("scalar", lambda h, xt=xt, stt=stt, hf=hf: h.activation(
                            out=junk[:], in_=xt[:, hf * 512:(hf + 1) * 512], func=AF.Square, accum_out=stt[:, 8 + hf:9 + hf]),
                            reads=[t_xt], writes=[t_junk, t_st])
                    fw.op("vector", lambda h, stt=stt: h.tensor_reduce(out=stt[:, 0:1], in_=stt[:, 8:16], axis=AX.X, op=ALU.add),
                          reads=[t_st], writes=[t_st])
                    fw.op("vector", lambda h, stt=stt: h.tensor_scalar(out=stt[:, 1:2], in0=stt[:, 0:1], scalar1=1.0 / D, scalar2=EPS,
                                                                       op0=ALU.mult, op1=ALU.add), reads=[t_st], writes=[t_st])
                    fw.op("scalar", lambda h, stt=stt: h.activation(out=stt[:, 2:3], in_=stt[:, 1:2], func=AF.Sqrt), reads=[t_st], writes=[t_st])
                    fw.op("vector", lambda h, stt=stt: h.reciprocal(out=stt[:, 3:4], in_=stt[:, 2:3]), reads=[t_st], writes=[t_st])
                    r2, t_r2 = rhs2_ring.next()
                    fw.op("vector", lambda h, r2=r2, stt=stt: h.tensor_scalar(out=r2[:, 128:256], in0=ident[:], scalar1=stt[:, 3:4], scalar2=None,
                                                                              op0=ALU.mult), reads=[t_st, t_const], writes=[t_r2])
                    for c2 in range(KC // 2):
                        if not halo and c2 % 4 == 0:
                            xTst, t_xTst = xTst_ring.next()
                        ps, t_ps = psAx.next()
                        for u in range(2):
                            c = c2 * 2 + u
                            if halo:
                                fw.op("tensor", lambda h, ps=ps, xt=xt, r2=r2, c=c, u=u: h.matmul(
                                    ps[:, u * 256 + 128:u * 256 + 256], lhsT=xt[:, c * 128:(c + 1) * 128], rhs=r2[:, 128:256], start=True, stop=True),
                                    reads=[t_xt, t_r2], writes=[t_ps], inc=(u == 1))
                            else:
                                fw.op("tensor", lambda h, ps=ps, xt=xt, r2=r2, c=c, u=u: h.matmul(
                                    ps[:, u * 256:(u + 1) * 256], lhsT=xt[:, c * 128:(c + 1) * 128], rhs=r2[:], start=True, stop=True),
                                    reads=[t_xt, t_r2], writes=[t_ps], inc=(u == 1))
                        ev_eng = "scalar" if c2 % 2 == 0 else "vector"
                        if not halo:
                            cq = (c2 % 4) * 2
                            if ev_eng == "scalar":
                                fw.op("scalar", lambda h, ps=ps, xTst=xTst, cq=cq: h.activation(
                                    out=xTst[:, cq:cq + 2, :], in_=ps[:].rearrange("p (u w) -> p u w", u=2)[:, :, 0:128], func=AF.Copy),
                                    reads=[t_ps], writes=[t_xTst])
                            else:
                                fw.op("vector", lambda h, ps=ps, xTst=xTst, cq=cq: h.tensor_copy(
                                    out=xTst[:, cq:cq + 2, :], in_=ps[:].rearrange("p (u w) -> p u w", u=2)[:, :, 0:128]),
                                    reads=[t_ps], writes=[t_xTst])
                        for u in range(2):
                            c = c2 * 2 + u
                            if ev_eng == "scalar":
                                fw.op("scalar", lambda h, ps=ps, c=c, u=u, tt=tt: h.activation(
                                    out=hT[:, c, tt * 128:(tt + 1) * 128], in_=ps[:, u * 256 + 128:u * 256 + 256], func=AF.Identity,
                                    scale=W1[:, c:c + 1], bias=sh1[:, c:c + 1]), reads=[t_ps] + tP, writes=[t_hT[tt]])
                            else:
                                fw.op("vector", lambda h, ps=ps, c=c, u=u, tt=tt: h.tensor_scalar(
                                    out=hT[:, c, tt * 128:(tt + 1) * 128], in0=ps[:, u * 256 + 128:u * 256 + 256],
                                    scalar1=W1[:, c:c + 1], scalar2=sh1[:, c:c + 1], op0=ALU.mult, op1=ALU.add),
                                    reads=[t_ps] + tP, writes=[t_hT[tt]])
                        if not halo and c2 % 4 == 3:
                            blk = r0 // 512
                            cb0 = (c2 // 4) * 8
                            fw.dma("sync", lambda h, xTst=xTst, r0=r0, cb0=cb0: h.dma_start(
                                out=xT_s[cb0:cb0 + 8, :, r0:r0 + 128].rearrange("c p t -> p c t"), in_=xTst[:]),
                                reads=[t_xTst], writes=[xT_t[c][blk] for c in range(cb0, cb0 + 8)])

            def inproj_group(col0, kind, hd0, tok0, halo):
                wt, t_wt = wt_ring.next()
                for hh in range(2):
                    fw.dma("gpsimd", lambda h, wt=wt, hh=hh: h.dma_start(
                        out=wt[:, hh * 16:(hh + 1) * 16, :],
                        in_=w_in[:, col0:col0 + 256].rearrange("(c p) n -> p c n", p=128)[:, hh * 16:(hh + 1) * 16, :]), writes=[t_wt])
                outs = []
                for sub in range(2):
                    hd = hd0 + sub
                    if kind == "xr":
                        xr, t_xr = xr_ring.next()
                        fw.op("vector", lambda h, xr=xr, hd=hd: h.tensor_copy(out=xr[:, 0:3], in_=hist[:, hd, :]), reads=[t_hs], writes=[t_xr])
                    elif kind == "yg":
                        gl, t_gl = gl_ring.next()
                    for blk in range(2):
                        ps, t_ps = psB.next()
                        for c in range(KC):
                            fw.op("tensor", lambda h, ps=ps, wt=wt, c=c, sub=sub, blk=blk: h.matmul(
                                ps[:], lhsT=wt[:, c, sub * 128:(sub + 1) * 128], rhs=hT[:, c, blk * 512:(blk + 1) * 512],
                                start=(c == 0), stop=(c == KC - 1)),
                                reads=[t_wt] + t_hT[blk * 4:blk * 4 + 4], writes=[t_ps], inc=(c == KC - 1))
                        if kind == "xr":
                            if halo:
                                fw.op("scalar", lambda h, ps=ps, xr=xr, blk=blk: h.activation(
                                    out=xr[:, 3 + blk * 512:3 + (blk + 1) * 512], in_=ps[:], func=AF.Copy, scale=flag[:, 0:1]),
                                    reads=[t_ps, t_par], writes=[t_xr])
                            else:
                                fw.op("scalar", lambda h, ps=ps, xr=xr, blk=blk: h.activation(
                                    out=xr[:, 3 + blk * 512:3 + (blk + 1) * 512], in_=ps[:], func=AF.Copy), reads=[t_ps], writes=[t_xr])
                        elif kind == "yg":
                            fw.op("scalar", lambda h, ps=ps, gl=gl, blk=blk: h.activation(
                                out=gl[:, blk * 512:(blk + 1) * 512], in_=ps[:], func=AF.Gelu), reads=[t_ps], writes=[t_gl])
                        else:
                            evb, t_ev = ev_ring.next()
                            if kind == "q":
                                fw.op("scalar", lambda h, ps=ps, evb=evb: h.activation(out=evb[:], in_=ps[:], func=AF.Copy, scale=QSCALE),
                                      reads=[t_ps], writes=[t_ev])
                                g = (tok0 + blk * 512) // 512
                                fw.dma("sync", lambda h, evb=evb, hd=hd, blk=blk: h.dma_start(
                                    out=q_s[hd, :, tok0 + blk * 512:tok0 + (blk + 1) * 512], in_=evb[:]), reads=[t_ev], writes=[q_t[hd][g]])
                            else:
                                fw.op("vector", lambda h, ps=ps, evb=evb: h.tensor_copy(out=evb[:], in_=ps[:]), reads=[t_ps], writes=[t_ev])
                                dst = k_s if kind == "k" else v_s
                                dtl = k_t if kind == "k" else v_t
                                base = (0 if halo else TH) + tok0 + blk * 512
                                fw.dma("sync", lambda h, evb=evb, hd=hd, dst=dst, base=base: h.dma_start(
                                    out=dst[hd, :, base:base + 512], in_=evb[:]), reads=[t_ev], writes=[dtl[hd][base // 512]])
                    if kind == "xr":
                        outs.append((xr, t_xr))
                    elif kind == "yg":
                        outs.append((gl, t_gl))
                return outs

            def adaln_half(j, half):
                wt, t_wt = wt_ring.next()
                col0 = j * 512 + half * 256
                for hh in range(2):
                    fw.dma("gpsimd", lambda h, wt=wt, hh=hh: h.dma_start(
                        out=wt[:, hh * 16:(hh + 1) * 16, :],
                        in_=ada_w[:, col0:col0 + 256].rearrange("(p c) n -> p c n", c=32)[:, hh * 16:(hh + 1) * 16, :]), writes=[t_wt])
                ps, t_ps = psB.next()
                for c in range(KC):
                    fw.op("tensor", lambda h, c=c: h.matmul(ps[:, 0:256], lhsT=crepB[:, c, :], rhs=wt[:, c, :], start=(c == 0), stop=(c == KC - 1)),
                          reads=[t_wt, t_crepB], writes=[t_ps], inc=(c == KC - 1))
                evb, t_ev = ev_ring.next()
                dtv = evb[:].bitcast(F32).rearrange("p (a m) -> p a m", a=2)
                fw.op("vector", lambda h: h.tensor_tensor(out=dtv, in0=ps[:, 0:256].rearrange("p (a m) -> p a m", a=2),
                                                          in1=ident[:].unsqueeze(1).to_broadcast([128, 2, 128]), op=ALU.mult),
                      reads=[t_ps, t_const], writes=[t_ev])
                c0 = 4 * j + 2 * half
                fw.op("vector", lambda h: h.tensor_reduce(out=modT[:, c0:c0 + 2], in_=dtv, axis=AX.X, op=ALU.add), reads=[t_ev], writes=[t_mod2])

            ada_todo = [(j, half) for j in range(16, 16 + N_ADA_B) for half in range(2)]

            def rec_conv(hd, xr, t_xr):
                fw.op("vector", lambda h: h.tensor_copy(out=hist[:, hd, :], in_=xr[:, 1024:1027]), reads=[t_xr], writes=[t_hs])
                R_ = [t_xr, t_rec] + tP
                fw.op("vector", lambda h: h.tensor_scalar(out=xc, in0=xr[:, 3:1027], scalar1=cw(3, hd), scalar2=P_sb[:, R_CB + hd:R_CB + hd + 1],
                                                          op0=ALU.mult, op1=ALU.add), reads=R_, writes=[t_rec], after=[t_XT[0]])
                for tap in (2, 1, 0):
                    fw.op("vector", lambda h, tap=tap: h.scalar_tensor_tensor(out=xc, in0=xr[:, tap:tap + 1024], scalar=cw(tap, hd), in1=xc,
                                                                              op0=ALU.mult, op1=ALU.add), reads=R_, writes=[t_rec])
                fw.op("scalar", lambda h: h.activation(out=xcb[:], in_=xc, func=AF.Copy), reads=[t_rec], writes=[t_rec])

            def rec_rest(hd, glp, tok0, halo):
                A1 = [t_XT[0], t_XT[1]]
                for blk in range(2):
                    sl = slice(blk * 512, (blk + 1) * 512)
                    psr, t_psr = psG.next()
                    fw.op("tensor", lambda h, psr=psr, sl=sl: h.matmul(psr[:], lhsT=wab[:, hd, :], rhs=xcb[:, sl], start=True, stop=True),
                          reads=[t_rec, t_wg], writes=[t_psr])
                    psi, t_psi = psG.next()
                    fw.op("tensor", lambda h, psi=psi, sl=sl: h.matmul(psi[:], lhsT=wib[:, hd, :], rhs=xcb[:, sl], start=True, stop=True),
                          reads=[t_rec, t_wg], writes=[t_psi])
                    fw.op("scalar", lambda h, psr=psr, sl=sl: h.activation(out=r_sb[:, sl], in_=psr[:], func=AF.Sigmoid,
                                                                            bias=P_sb[:, R_BA + hd:R_BA + hd + 1]), reads=[t_psr] + tP, writes=[t_rec], after=A1)
                    fw.op("scalar", lambda h, psi=psi, sl=sl: h.activation(out=i_sb[:, sl], in_=psi[:], func=AF.Sigmoid,
                                                                            bias=P_sb[:, R_BI + hd:R_BI + hd + 1]), reads=[t_psi] + tP, writes=[t_rec])
                fw.op("scalar", lambda h: h.activation(out=a_sb, in_=r_sb, func=AF.Exp, scale=m8sp[:, hd:hd + 1]), reads=[t_rec] + tP, writes=[t_rec])
                fw.op("scalar", lambda h: h.activation(out=q_sb, in_=r_sb, func=AF.Exp, scale=m16sp[:, hd:hd + 1]), reads=[t_rec] + tP, writes=[t_rec])
                fw.op("vector", lambda h: h.tensor_tensor(out=i_sb, in0=i_sb, in1=xc, op=ALU.mult), reads=[t_rec], writes=[t_rec])
                fw.op("scalar", lambda h: h.activation(out=q_sb, in_=q_sb, func=AF.Sqrt, scale=-1.0, bias=1.0), reads=[t_rec], writes=[t_rec])
                if halo:
                    fw.op("vector", lambda h: h.scalar_tensor_tensor(out=i_sb, in0=i_sb, scalar=flag[:, 0:1], in1=q_sb,
                                                                     op0=ALU.mult, op1=ALU.mult), reads=[t_rec, t_par], writes=[t_rec])
                else:
                    fw.op("vector", lambda h: h.tensor_tensor(out=i_sb, in0=i_sb, in1=q_sb, op=ALU.mult), reads=[t_rec], writes=[t_rec])
                hi = hs_i[0]; hs_i[0] ^= 1
                hs = hs_bufs[hi]; t_h = t_hsb[hi]
                fw.op("vector", lambda h: h.tensor_tensor_scan(out=hs, data0=a_sb, data1=i_sb, initial=state[:, hd:hd + 1],
                                                               op0=ALU.mult, op1=ALU.add), reads=[t_rec, t_hs], writes=[t_h], after=A1)
                fw.op("vector", lambda h: h.tensor_copy(out=state[:, hd:hd + 1], in_=hs[:, 1023:1024]), reads=[t_h], writes=[t_hs])
                if not halo:
                    gl, t_gl = glp
                    fw.op("vector", lambda h: h.tensor_tensor(out=hs, in0=hs, in1=gl[:], op=ALU.mult), reads=[t_h, t_gl], writes=[t_h])
                    g = tok0 // 512
                    fw.dma("sync", lambda h: h.dma_start(out=mix_s[hd, :, tok0:tok0 + 1024], in_=hs),
                           reads=[t_h], writes=[mix_t[hd][g], mix_t[hd][g + 1]])

            for (src, halo, sti) in ((xh, True, 0), (xh, True, 1), (xm, False, 0), (xm, False, 1))[OPTS.get("st0", 0):OPTS.get("st1", 4)]:
                tok0 = sti * 1024
                phase_a(src, tok0, halo)
                if OPTS.get("only_a"):
                    continue
                pend = None
                for hp in range(9):
                    xrs = inproj_group(hp * 256, "xr", hp * 2, tok0, halo) if hp < 8 else None
                    if pend is not None:
                        php, pxrs = pend
                        rec = not OPTS.get("no_rec")
                        if rec:
                            rec_conv(php * 2, pxrs[0][0], pxrs[0][1])
                        gls = inproj_group(2048 + php * 256, "yg", php * 2, tok0, halo) if not halo else [None, None]
                        if rec:
                            rec_rest(php * 2, gls[0], tok0, halo)
                            rec_conv(php * 2 + 1, pxrs[1][0], pxrs[1][1])
                        if not halo:
                            inproj_group(4096 + php * 256, "q", php * 2, tok0, halo)
                        inproj_group(6144 + php * 256, "k", php * 2, tok0, halo)
                        if rec:
                            rec_rest(php * 2 + 1, gls[1], tok0, halo)
                        inproj_group(8192 + php * 256, "v", php * 2, tok0, halo)
                        if ada_todo and hp in (1, 2, 3, 5, 6, 7):
                            adaln_half(*ada_todo.pop(0))
                    pend = (hp, xrs)
            while ada_todo:
                adaln_half(*ada_todo.pop(0))
            fw.barrier()

        if stop_after == 1:
            fin = [t for l in (q_t, k_t, v_t, mix_t, xT_t) for r in l for t in r]
            return _finish(nc, fw, y_out, st, debug, fin)

        def tok_ap(buf, base, stride):
            if stride == 1:
                return buf[:, base:base + 128]
            res = base % stride
            b0 = base - res
            return buf[:, b0:b0 + 128 * stride].rearrange("p (i r) -> p r i", r=stride)[:, res, :]

        with ExitStack() as ph:
            qh_ring = Ring([sb("qh%d" % i, [128, T], BF16, ph) for i in range(2)])
            kh_ring = Ring([sb("kh%d" % i, [128, TH + T], BF16, ph) for i in range(2)])
            vh_ring = Ring([sb("vh%d" % i, [128, TH + T], BF16, ph) for i in range(2)])
            bt_ring = Ring([sb("bt%d" % i, [128, 2, 3, 256], BF16, ph) for i in range(2)])
            Vt_ring = Ring([sb("Vt%d" % i, [128, 72, 128], BF16, ph) for i in range(2)])
            E_ring = Ring([sb("E%d" % i, [128, 512], BF16, ph) for i in range(4)])
            accO_ring = Ring([sb("accO%d" % i, [128, T], F32, ph) for i in range(2)])
            accD_ring = Ring([sb("accD%d" % i, [128, T], F32, ph) for i in range(2)])
            psS = Ring([pp("psS%d" % i, [128, 512], F32, ph) for i in range(4)], excl=True)
            psO = Ring([pp("psO%d" % i, [128, 512], F32, ph) for i in range(2)], excl=True)
            psD = Ring([pp("psD%d" % i, [128, 512], F32, ph) for i in range(1)], excl=True)
            psV = Ring([pp("psV%d" % i, [128, 512], F32, ph) for i in range(1)], excl=True)

            vslots = []
            for n in range(-1, 16):
                vslots.append((TH + 128 * n, 1))
            for n in range(-1, 4):
                for r in range(4):
                    vslots.append((TH + 512 * n + r, 4))
            for n in range(-1, 1):
                for r in range(16):
                    vslots.append((TH + 2048 * n + r, 16))
            groups = []
            for g in range(4):
                us = []
                for n in range(4 * g, 4 * g + 4):
                    us.append((128 * n, 1, TH + 128 * (n - 1), TH + 128 * n, n, n + 1, n == 0))
                groups.append((0, us, (lambda acc, g=g: acc[:, 512 * g:512 * g + 512].rearrange("p (n i) -> p n i", n=4))))
            for n in range(4):
                us = []
                for r in range(4):
                    us.append((512 * n + r, 4, TH + 512 * (n - 1) + r, TH + 512 * n + r, 17 + n * 4 + r, 17 + (n + 1) * 4 + r, n == 0))
                groups.append((1, us, (lambda acc, n=n: acc[:, 512 * n:512 * n + 512].rearrange("p (i r) -> p r i", r=4))))
            for g in range(4):
                us = []
                for r in range(4 * g, 4 * g + 4):
                    us.append((r, 16, r, TH + r, 37 + r, 37 + 16 + r, True))
                groups.append((2, us, (lambda acc, g=g: acc[:, :].rearrange("p (i r) -> p r i", r=16)[:, 4 * g:4 * g + 4, :])))

            crep2 = sb("crep2", [128, 32, 128], BF16, ph); t_crep2 = Tile()
            fw.op("vector", lambda h: h.tensor_copy(out=crep2[:], in_=cs[:].unsqueeze(2).to_broadcast([128, 32, 128])),
                  reads=[t_cs], writes=[t_crep2])
            wa_ring2 = Ring([sb("wb%d" % i, [128, 32, 512], BF16, ph) for i in range(2)])
            dtmp2 = Ring([sb("dtmq%d" % i, [128, 4, 128], F32, ph) for i in range(2)])

            def load_head(hd):
                qh, t_qh = qh_ring.next(); kh, t_kh = kh_ring.next(); vh, t_vh = vh_ring.next()
                bt, t_bt = bt_ring.next(); Vt, t_Vt = Vt_ring.next()
                fw.dma("sync", lambda h: h.dma_start(out=qh[:], in_=q_s[hd]), reads=q_t[hd], writes=[t_qh])
                fw.dma("sync", lambda h: h.dma_start(out=kh[:], in_=k_s[hd]), reads=k_t[hd], writes=[t_kh])
                fw.dma("sync", lambda h: h.dma_start(out=vh[:], in_=v_s[hd]), reads=v_t[hd], writes=[t_vh])
                fw.dma("gpsimd", lambda h: h.dma_start(out=bt[:, 0, :, :], in_=btab[:, hd].rearrange("p k c -> k p c")), writes=[t_bt])
                fw.dma("gpsimd", lambda h: h.dma_start(out=bt[:, 1, :, :], in_=btabe[:, hd].rearrange("p k c -> k p c")), writes=[t_bt])
                return (qh, t_qh, kh, t_kh, vh, t_vh, bt, t_bt, Vt, t_Vt)

            def vtrans(bufs):
                (qh, t_qh, kh, t_kh, vh, t_vh, bt, t_bt, Vt, t_Vt) = bufs
                for s0 in range(0, 69, 8):
                    n = min(8, 69 - s0)
                    pvf, t_pv = psV.next()
                    pv = pvf[:].bitcast(BF16)
                    for i in range(n):
                        base, stride = vslots[s0 + i]
                        fw.op("tensor", lambda h, i=i, base=base, stride=stride: h.transpose(
                            pv[:, i * 128:(i + 1) * 128], tok_ap(vh, base, stride), identb[:]),
                            reads=[t_vh, t_const], writes=[t_pv], inc=(i == n - 1))
                    if (s0 // 8) % 2 == 0:
                        fw.op("vector", lambda h: h.tensor_copy(
                            out=Vt[:, s0:s0 + n, :], in_=pv[:, 0:n * 128].rearrange("p (a d) -> p a d", d=128)), reads=[t_pv], writes=[t_Vt])
                    else:
                        fw.op("scalar", lambda h: h.activation(
                            out=Vt[:, s0:s0 + n, :], in_=pv[:, 0:n * 128].rearrange("p (a d) -> p a d", d=128), func=AF.Copy), reads=[t_pv], writes=[t_Vt])

            ada_c = list(range(16 + N_ADA_B, 48))
            n_two = len(ada_c) - 16
            nxt = load_head(0)
            vtrans(nxt)
            for hd in range(16):
                (qh, t_qh, kh, t_kh, vh, t_vh, bt, t_bt, Vt, t_Vt) = nxt
                accO, t_accO = accO_ring.next(); accD, t_accD = accD_ring.next()
                if hd + 1 < 16:
                    nxt = load_head(hd + 1)

                def stage_s(gi):
                    p, us, _ = groups[gi]
                    Es = []
                    for pair in range(2):
                        ps, t_ps = psS.next()
                        for w in range(2):
                            qb, strd, kpb, kcb, vp, vc, edge = us[pair * 2 + w]
                            reg = ps[:, w * 256:(w + 1) * 256]
                            qa = tok_ap(qh, qb, strd)
                            fw.op("tensor", lambda h, reg=reg, e=(1 if edge else 0), p=p: h.matmul(
                                reg, lhsT=identb[:], rhs=bt[:, e, p, :], start=True, stop=False),
                                reads=[t_bt, t_const], writes=[t_ps], inc=False)
                            fw.op("tensor", lambda h, reg=reg, kpb=kpb, strd=strd, qa=qa: h.matmul(
                                reg[:, 0:128], lhsT=tok_ap(kh, kpb, strd), rhs=qa, start=False, stop=False),
                                reads=[t_kh, t_qh], writes=[t_ps], inc=False)
                            fw.op("tensor", lambda h, reg=reg, kcb=kcb, strd=strd, qa=qa: h.matmul(
                                reg[:, 128:256], lhsT=tok_ap(kh, kcb, strd), rhs=qa, start=False, stop=True),
                                reads=[t_kh, t_qh], writes=[t_ps], inc=(w == 1))
                        E, t_E = E_ring.next()
                        fw.op("scalar", lambda h, ps=ps, E=E: h.activation(out=E[:], in_=ps[:], func=AF.Exp), reads=[t_ps], writes=[t_E])
                        Es.append((E, t_E))
                    return Es

                def stage_pv(gi, Es):
                    p, us, accv = groups[gi]
                    po, t_po = psO.next(); pd, t_pd = psD.next()
                    for ui in range(4):
                        qb, strd, kpb, kcb, vp, vc, edge = us[ui]
                        E, t_E = Es[ui // 2]
                        w = ui % 2
                        oreg = po[:, ui * 128:(ui + 1) * 128]
                        dreg = pd[:, ui * 128:(ui + 1) * 128]
                        fw.op("tensor", lambda h, oreg=oreg, vp=vp, E=E, w=w: h.matmul(
                            oreg, lhsT=Vt[:, vp, :], rhs=E[:, w * 256:w * 256 + 128], start=True, stop=False),
                            reads=[t_Vt, t_E], writes=[t_po], inc=False)
                        fw.op("tensor", lambda h, oreg=oreg, vc=vc, E=E, w=w: h.matmul(
                            oreg, lhsT=Vt[:, vc, :], rhs=E[:, w * 256 + 128:w * 256 + 256], start=False, stop=True),
                            reads=[t_Vt, t_E], writes=[t_po], inc=(ui == 3))
                        fw.op("tensor", lambda h, dreg=dreg, E=E, w=w: h.matmul(
                            dreg, lhsT=onesb[:], rhs=E[:, w * 256:w * 256 + 128], start=True, stop=False),
                            reads=[t_const, t_E], writes=[t_pd], inc=False)
                        fw.op("tensor", lambda h, dreg=dreg, E=E, w=w: h.matmul(
                            dreg, lhsT=onesb[:], rhs=E[:, w * 256 + 128:w * 256 + 256], start=False, stop=True),
                            reads=[t_const, t_E], writes=[t_pd], inc=(ui == 3))
                    pov = po[:].rearrange("p (u i) -> p u i", u=4)
                    pdv = pd[:].rearrange("p (u i) -> p u i", u=4)
                    if p == 0:
                        fw.op("vector", lambda h, pov=pov: h.tensor_copy(out=accv(accO), in_=pov), reads=[t_po], writes=[t_accO])
                        fw.op("vector", lambda h, pdv=pdv: h.tensor_copy(out=accv(accD), in_=pdv), reads=[t_pd], writes=[t_accD])
                    else:
                        fw.op("vector", lambda h, pov=pov: h.tensor_tensor(out=accv(accO), in0=pov, in1=accv(accO), op=ALU.add),
                              reads=[t_po, t_accO], writes=[t_accO])
                        fw.op("vector", lambda h, pdv=pdv: h.tensor_tensor(out=accv(accD), in0=pdv, in1=accv(accD), op=ALU.add),
                              reads=[t_pd, t_accD], writes=[t_accD])

                prev = None
                for gi in range(len(groups)):
                    Es = stage_s(gi)
                    if prev is not None:
                        stage_pv(prev[0], prev[1])
                    prev = (gi, Es)
                stage_pv(prev[0], prev[1])
                if hd + 1 < 16:
                    vtrans(nxt)
                for _ in range(2 if hd < n_two else 1):
                    pvf, t_pv = psV.next()
                    adaln_chunk(ada_c.pop(0), crep2, t_crep2, wa_ring2, pvf, t_pv, dtmp2, t_mod2)
                fw.op("vector", lambda h, accD=accD: h.reciprocal(out=accD[:], in_=accD[:]), reads=[t_accD], writes=[t_accD])
                fw.op("vector", lambda h, accO=accO, accD=accD: h.tensor_tensor(out=accO[:], in0=accO[:], in1=accD[:], op=ALU.mult),
                      reads=[t_accO, t_accD], writes=[t_accO])
                fw.dma("sync", lambda h, accO=accO, hd=hd: h.dma_start(out=mix_s[16 + hd], in_=accO[:]), reads=[t_accO], writes=mix_t[16 + hd])
            fw.op("vector", lambda h: h.tensor_tensor(out=modT[:, 64:192], in0=modT[:, 64:192], in1=P_sb[:, R_ADAB + 64:R_ADAB + 192], op=ALU.add),
                  reads=[t_mod2, t_par], writes=[t_mod2])
            fw.op("vector", lambda h: h.scalar_tensor_tensor(out=W2[:], in0=modT[:, 128:160], scalar=1.0, in1=P_sb[:, R_N2G:R_N2G + 32],
                                                             op0=ALU.add, op1=ALU.mult), reads=[t_mod2, t_par], writes=[t_mod2])
            if debug:
                fw.dma("sync", lambda h: h.dma_start(out=mod_dbg, in_=modT[:]), reads=[t_mod, t_mod2], writes=[t_md])
            fw.barrier()

        if stop_after == 2:
            fin = [t for r in mix_t for t in r]
            return _finish(nc, fw, y_out, st, debug, fin)

        FP = 16
        passes = [(f0, min(FP, NFC - f0)) for f0 in range(0, NFC, FP)]
        y_t = [Tile() for _ in range(16)]
        for tb in range(2):
            T0 = tb * 1024
            sb2 = lambda name, shape, dt=F32, stack=None, tb=tb: sb("%s_t%d" % (name, tb), shape, dt, stack)
            pp2 = lambda name, shape, dt=F32, stack=None, tb=tb: pp("%s_t%d" % (name, tb), shape, dt, stack)
            with ExitStack() as ph:
                mh = sb2("mh", [128, KC, 1024], BF16, ph); t_mh = [Tile(), Tile()]
                ld_ring = Ring([sb2("ld%d" % i, [128, 512], F32, ph) for i in range(4)])
                tmp_ring = Ring([sb2("tmp%d" % i, [128, 512], F32, ph) for i in range(2)])
                x1_ring = Ring([sb2("x1_%d" % i, [128, 512], F32, ph) for i in range(2)])
                sq_ring = Ring([sb2("sqb%d" % i, [128, 512], BF16, ph) for i in range(4)])
                rs = [sb2("rs%d" % i, [128, 512], F32, ph) for i in range(2)]; t_rs = [Tile(), Tile()]
                wt_big = sb2("wtbig", [128, 4, KC, 256], BF16, ph)
                wt_ring = Ring([wt_big[:, i] for i in range(4)])
                actT = sb2("actT", [128, FP, 1024], BF16, ph); t_act = [Tile(), Tile()]
                wdn_ring = Ring([sb2("wdn%d" % i, [128, FP, 256], BF16, ph) for i in range(2)])
                psq = Ring([pp2("psq%d" % i, [128, 512], F32, ph) for i in range(2)], excl=True)
                psP = Ring([pp2("psP%d" % i, [128, 512], F32, ph) for i in range(2)], excl=True)
                psGU = Ring([pp2("psGU%d" % i, [128, 512], F32, ph) for i in range(4)], excl=True)
                psPx = Ring([])
                psPx.items = psP.items + psGU.items

                def rstd_from(ps, t_ps, dst, t_dst, n):
                    fw.op("vector", lambda h: h.tensor_scalar(out=dst[:], in0=ps[:], scalar1=1.0 / n, scalar2=EPS, op0=ALU.mult, op1=ALU.add),
                          reads=[t_ps], writes=[t_dst])
                    fw.op("scalar", lambda h: h.activation(out=dst[:], in_=dst[:], func=AF.Sqrt), reads=[t_dst], writes=[t_dst])
                    fw.op("vector", lambda h: h.reciprocal(out=dst[:], in_=dst[:]), reads=[t_dst], writes=[t_dst])

                pend_sq = []

                def emit_ssq(item):
                    (pq, t_pq), sq, t_sq, ncn = item
                    fw.op("tensor", lambda h: h.matmul(pq[:], lhsT=onesb[:], rhs=sq[:], start=(ncn == 0), stop=(ncn == KC - 1)),
                          reads=[t_sq, t_const], writes=[t_pq], inc=True)

                stage = wt_big[:].rearrange("p a c n -> p (a c n)").bitcast(F32).rearrange("p (h c t) -> p h c t", h=2, c=16)
                t_wts4 = [wt_ring.items[i][1] for i in range(4)]
                t_stq = [[Tile() for _ in range(4)] for _ in range(2)]
                stq_all = [t for r in t_stq for t in r]
                it = 0
                for blk in range(2):
                    tk0 = T0 + blk * 512
                    g4 = tk0 // 512
                    for grp in range(2):
                        half = it % 2
                        it += 1
                        for q4 in range(4):
                            c0 = grp * 16 + q4 * 4
                            fw.dma("sync", lambda h, half=half, q4=q4, c0=c0: h.dma_start(
                                out=stage[:, half, q4 * 4:(q4 + 1) * 4, :], in_=mix_s[c0:c0 + 4, :, tk0:tk0 + 512].rearrange("c p t -> p c t")),
                                reads=[mix_t[c0 + i][g4] for i in range(4)], writes=[t_stq[half][q4]], after=t_wts4[2 * half:2 * half + 2])
                        ps, t_ps = psq.next()
                        for c in range(16):
                            sq, t_sq = sq_ring.next()
                            fw.op("scalar", lambda h, sq=sq, half=half, c=c: h.activation(out=sq[:], in_=stage[:, half, c, :], func=AF.Square),
                                  reads=[t_stq[half][c // 4]], writes=[t_sq])
                            fw.op("tensor", lambda h, ps=ps, sq=sq, c=c: h.matmul(ps[:], lhsT=onesb[:], rhs=sq[:], start=(c == 0), stop=(c == 15)),
                                  reads=[t_sq, t_const], writes=[t_ps], inc=True)
                        rstd_from(ps, t_ps, rs[grp], t_rs[grp], 2048.0)
                        for c in range(16):
                            ch = grp * 16 + c
                            gcol = (R_GR if grp == 0 else R_GA) + c
                            fw.op("vector", lambda h, half=half, c=c, ch=ch, gcol=gcol, grp=grp: h.scalar_tensor_tensor(
                                out=mh[:, ch, blk * 512:(blk + 1) * 512], in0=stage[:, half, c, :], scalar=P_sb[:, gcol:gcol + 1], in1=rs[grp][:],
                                op0=ALU.mult, op1=ALU.mult), reads=[t_stq[half][c // 4], t_rs[grp]] + tP2, writes=[t_mh[blk]])

                psn = [psq.next(), psq.next()]
                grp_list = [(nc2 * 2 + sub, blk) for nc2 in range(16) for sub in range(2) for blk in range(2)]
                ld_of = {}

                def issue_ld(i):
                    if i < len(grp_list) and i not in ld_of:
                        ncn_, blk_ = grp_list[i]
                        tk_ = T0 + blk_ * 512
                        ld, t_ld = ld_ring.next()
                        fw.dma("sync", lambda h: h.dma_start(out=ld[:], in_=xT_s[ncn_, :, tk_:tk_ + 512]),
                               reads=[xT_t[ncn_][tk_ // 512]], writes=[t_ld])
                        ld_of[i] = (ld, t_ld)

                issue_ld(0); issue_ld(1)
                gi = 0
                for nc2 in range(16):
                    wt, t_wt = wt_ring.next()
                    for hh in range(2):
                        fw.dma("gpsimd", lambda h, wt=wt, hh=hh, nc2=nc2: h.dma_start(
                            out=wt[:, hh * 16:(hh + 1) * 16, :],
                            in_=w_out[:, nc2 * 256:(nc2 + 1) * 256].rearrange("(c p) n -> p c n", p=128)[:, hh * 16:(hh + 1) * 16, :]), writes=[t_wt],
                            after=(stq_all if nc2 < 4 else ()))
                    for sub in range(2):
                        ncn = nc2 * 2 + sub
                        for blk in range(2):
                            tk0 = T0 + blk * 512
                            g4 = tk0 // 512
                            issue_ld(gi + 2)
                            ps, t_ps = psPx.next()
                            for c in range(KC):
                                fw.op("tensor", lambda h, ps=ps, wt=wt, c=c, sub=sub, blk=blk: h.matmul(
                                    ps[:], lhsT=wt[:, c, sub * 128:(sub + 1) * 128], rhs=mh[:, c, blk * 512:(blk + 1) * 512],
                                    start=(c == 0), stop=(c == KC - 1)), reads=[t_wt, t_mh[blk]], writes=[t_ps], inc=(c == KC - 1))
                            ld, t_ld = ld_of.pop(gi)
                            gi += 1
                            tmp, t_tmp = tmp_ring.next()
                            fw.op("scalar", lambda h, ps=ps, tmp=tmp, ncn=ncn: h.activation(out=tmp[:], in_=ps[:], func=AF.Copy, scale=g1[:, ncn:ncn + 1]),
                                  reads=[t_ps] + tP2, writes=[t_tmp])
                            x1, t_x1 = x1_ring.next()
                            fw.op("vector", lambda h, tmp=tmp, ld=ld, x1=x1: h.tensor_tensor(out=x1[:], in0=tmp[:], in1=ld[:], op=ALU.add),
                                  reads=[t_tmp, t_ld], writes=[t_x1])
                            fw.dma("sync", lambda h, x1=x1, ncn=ncn, tk0=tk0: h.dma_start(out=xT_s[ncn, :, tk0:tk0 + 512], in_=x1[:]),
                                   reads=[t_x1], writes=[xT_t[ncn][g4]])
                            sq, t_sq = sq_ring.next()
                            fw.op("scalar", lambda h, x1=x1, sq=sq: h.activation(out=sq[:], in_=x1[:], func=AF.Square), reads=[t_x1], writes=[t_sq])
                            pend_sq.append((psn[blk], sq, t_sq, ncn))
                            while len(pend_sq) > 2:
                                emit_ssq(pend_sq.pop(0))
                while pend_sq:
                    emit_ssq(pend_sq.pop(0))
                for blk in range(2):
                    rstd_from(psn[blk][0], psn[blk][1], rs[blk], t_rs[blk], float(D))
                for ncn in range(KC):
                    for blk in range(2):
                        tk0 = T0 + blk * 512
                        g4 = tk0 // 512
                        ld, t_ld = ld_ring.next()
                        fw.dma("sync", lambda h, ld=ld, ncn=ncn, tk0=tk0: h.dma_start(out=ld[:], in_=xT_s[ncn, :, tk0:tk0 + 512]),
                               reads=[xT_t[ncn][g4]], writes=[t_ld])
                        tmp, t_tmp = tmp_ring.next()
                        fw.op("vector", lambda h, ld=ld, tmp=tmp, ncn=ncn, blk=blk: h.scalar_tensor_tensor(
                            out=tmp[:], in0=ld[:], scalar=W2[:, ncn:ncn + 1], in1=rs[blk][:], op0=ALU.mult, op1=ALU.mult),
                            reads=[t_ld, t_rs[blk]] + tP2, writes=[t_tmp])
                        fw.op("scalar", lambda h, tmp=tmp, ncn=ncn, blk=blk: h.activation(
                            out=mh[:, ncn, blk * 512:(blk + 1) * 512], in_=tmp[:], func=AF.Identity, bias=sh2[:, ncn:ncn + 1]),
                            reads=[t_tmp] + tP2, writes=[t_mh[blk]])

                if stop_after == 3:
                    fw.barrier()
                    continue

                for pi, (f0, nf) in enumerate(passes):
                    last = pi == len(passes) - 1
                    for fp in range(nf // 2):
                        fcol = (f0 + 2 * fp) * 128
                        wg, t_wg2 = wt_ring.next(); wu, t_wu = wt_ring.next()
                        for (wsrc, wdst, t_w) in ((w_gate, wg, t_wg2), (w_up, wu, t_wu)):
                            for hh in range(2):
                                fw.dma("gpsimd", lambda h, wsrc=wsrc, wdst=wdst, hh=hh, fcol=fcol: h.dma_start(
                                    out=wdst[:, hh * 16:(hh + 1) * 16, :],
                                    in_=wsrc[:, fcol:fcol + 256].rearrange("(c p) n -> p c n", p=128)[:, hh * 16:(hh + 1) * 16, :]), writes=[t_w])
                        for sub in range(2):
                            fl = 2 * fp + sub
                            for blk in range(2):
                                pg, t_pg = psGU.next(); pu, t_pu = psGU.next()
                                for (wsb, t_w, pdst, t_pd) in ((wg, t_wg2, pg, t_pg), (wu, t_wu, pu, t_pu)):
                                    for c in range(KC):
                                        fw.op("tensor", lambda h, pdst=pdst, wsb=wsb, c=c, sub=sub, blk=blk: h.matmul(
                                            pdst[:], lhsT=wsb[:, c, sub * 128:(sub + 1) * 128], rhs=mh[:, c, blk * 512:(blk + 1) * 512],
                                            start=(c == 0), stop=(c == KC - 1)), reads=[t_w, t_mh[blk]], writes=[t_pd], inc=(c == KC - 1))
                                tmp, t_tmp = tmp_ring.next()
                                fw.op("scalar", lambda h, pg=pg, tmp=tmp: h.activation(out=tmp[:], in_=pg[:], func=AF.Silu), reads=[t_pg], writes=[t_tmp])
                                fw.op("vector", lambda h, tmp=tmp, pu=pu, fl=fl, blk=blk: h.tensor_tensor(
                                    out=actT[:, fl, blk * 512:(blk + 1) * 512], in0=tmp[:], in1=pu[:], op=ALU.mult),
                                    reads=[t_tmp, t_pu], writes=[t_act[blk]])
                    if last:
                        psn = [psq.next(), psq.next()]
                    ld_of.clear()
                    issue_ld(0); issue_ld(1)
                    gi = 0
                    for nc2 in range(16):
                        wdn, t_wdn = wdn_ring.next()
                        fw.dma("gpsimd", lambda h, wdn=wdn, nc2=nc2, f0=f0, nf=nf: h.dma_start(
                            out=wdn[:, 0:nf, :],
                            in_=w_down[f0 * 128:(f0 + nf) * 128, nc2 * 256:(nc2 + 1) * 256].rearrange("(c p) n -> p c n", p=128)), writes=[t_wdn])
                        for sub in range(2):
                            ncn = nc2 * 2 + sub
                            for blk in range(2):
                                tk0 = T0 + blk * 512
                                g4 = tk0 // 512
                                issue_ld(gi + 2)
                                ps, t_ps = psPx.next()
                                for c in range(nf):
                                    fw.op("tensor", lambda h, ps=ps, wdn=wdn, c=c, sub=sub, blk=blk: h.matmul(
                                        ps[:], lhsT=wdn[:, c, sub * 128:(sub + 1) * 128], rhs=actT[:, c, blk * 512:(blk + 1) * 512],
                                        start=(c == 0), stop=(c == nf - 1)), reads=[t_wdn, t_act[blk]], writes=[t_ps], inc=(c == nf - 1))
                                ld, t_ld = ld_of.pop(gi)
                                gi += 1
                                tmp, t_tmp = tmp_ring.next()
                                fw.op("scalar", lambda h, ps=ps, tmp=tmp, ncn=ncn: h.activation(out=tmp[:], in_=ps[:], func=AF.Copy, scale=g2[:, ncn:ncn + 1]),
                                      reads=[t_ps] + tP2, writes=[t_tmp])
                                x1, t_x1 = x1_ring.next()
                                fw.op("vector", lambda h, tmp=tmp, ld=ld, x1=x1: h.tensor_tensor(out=x1[:], in0=tmp[:], in1=ld[:], op=ALU.add),
                                      reads=[t_tmp, t_ld], writes=[t_x1])
                                fw.dma("sync", lambda h, x1=x1, ncn=ncn, tk0=tk0: h.dma_start(out=xT_s[ncn, :, tk0:tk0 + 512], in_=x1[:]),
                                       reads=[t_x1], writes=[xT_t[ncn][g4]])
                                if last:
                                    sq, t_sq = sq_ring.next()
                                    fw.op("scalar", lambda h, x1=x1, sq=sq: h.activation(out=sq[:], in_=x1[:], func=AF.Square), reads=[t_x1], writes=[t_sq])
                                    pend_sq.append((psn[blk], sq, t_sq, ncn))
                                    while len(pend_sq) > 2:
                                        emit_ssq(pend_sq.pop(0))
                while pend_sq:
                    emit_ssq(pend_sq.pop(0))
                for blk in range(2):
                    rstd_from(psn[blk][0], psn[blk][1], rs[blk], t_rs[blk], float(D))
                ost_dummy = None
                ost = wt_big[:].rearrange("p a c n -> p (a c n)").bitcast(F32).rearrange("p (j n) -> p j n", j=4)
                t_wts = [wt_ring.items[i][1] for i in range(4)]
                t_ostE = Tile(); t_ostO = Tile()
                for blk in range(2):
                    tk0 = T0 + blk * 512
                    g4 = tk0 // 512
                    for ncn in range(KC):
                        ld, t_ld = ld_ring.next()
                        fw.dma("sync", lambda h, ld=ld, ncn=ncn, tk0=tk0: h.dma_start(out=ld[:], in_=xT_s[ncn, :, tk0:tk0 + 512]),
                               reads=[xT_t[ncn][g4]], writes=[t_ld])
                        tmp, t_tmp = tmp_ring.next()
                        fw.op("vector", lambda h, ld=ld, tmp=tmp, ncn=ncn, blk=blk: h.scalar_tensor_tensor(
                            out=tmp[:], in0=ld[:], scalar=P_sb[:, R_FG + ncn:R_FG + ncn + 1], in1=rs[blk][:], op0=ALU.mult, op1=ALU.mult),
                            reads=[t_ld, t_rs[blk]] + tP2, writes=[t_tmp])
                        pt, t_pt = psP.next()
                        for j in range(4):
                            fw.op("tensor", lambda h, pt=pt, tmp=tmp, j=j: h.transpose(pt[:, j * 128:(j + 1) * 128], tmp[:, j * 128:(j + 1) * 128], ident[:]),
                                  reads=[t_tmp, t_const], writes=[t_pt], inc=(j == 3))
                        ptv = pt[:].rearrange("p (j n) -> p j n", j=4)
                        fw.op("scalar", lambda h, ptv=ptv, ncn=ncn: h.activation(out=ost[:, :, ncn * 128:(ncn + 1) * 128], in_=ptv, func=AF.Copy),
                              reads=[t_pt], writes=[t_ostE], after=t_wts)
                    for j in range(4):
                        r0 = tk0 + j * 128
                        fw.dma("sync", lambda h, j=j, r0=r0: h.dma_start(out=y_out[r0:r0 + 128, :], in_=ost[:, j, :]), reads=[t_ostE, t_ostO], writes=[y_t[r0 // 128]])
                fw.barrier()

        return _finish(nc, fw, y_out, st, debug, y_t + ([t for r in xT_t for t in r] if debug else []))


def _finish(nc, fw, y_out, st, debug, tiles):
    fw.wait_all("sync", tiles)
    return nc


_PROGRAM = None
OPTS = {}


def kernel(**inputs):
    global _PROGRAM
    maps = _prep_inputs(inputs)
    if _PROGRAM is None:
        _PROGRAM = build_program()
    res = run_bass_kernel_spmd(_PROGRAM, maps, core_ids=list(range(8)))
    out = np.empty((4, 4096, D), np.float32)
    for b in range(4):
        for s in range(2):
            out[b, s * T:(s + 1) * T] = res.results[b * 2 + s]["y"]
    return out
```

```python
import math
from contextlib import ExitStack

import numpy as np
import concourse.bass as bass
import concourse.mybir as mybir
from concourse.bass_utils import run_bass_kernel_spmd

F32 = mybir.dt.float32
BF16 = mybir.dt.bfloat16
AF = mybir.ActivationFunctionType
ALU = mybir.AluOpType
AX = mybir.AxisListType

D = 4096
KC = 32
T = 2048
TH = 2048
NCOL = 10240
DFF = 11008
NFC = DFF // 128
EPS = 1e-6
NEG = -1e30
QSCALE = 128 ** -0.5

R_N1G, R_N2G, R_FG, R_CW, R_CB, R_BA, R_BI, R_LAM, R_GR, R_GA, R_ADAB, R_TOT = (
    0, 32, 64, 96, 160, 176, 192, 208, 224, 240, 256, 448)


class Tile:
    __slots__ = ("w", "readers", "excl")

    def __init__(self, excl=False):
        self.w = None
        self.readers = {}
        self.excl = excl


class Eng:
    def __init__(self, name, handle, sem):
        self.name = name
        self.handle = handle
        self.sem = sem
        self.count = 0
        self.waited = {}
        self.same_wait = name != "tensor"
        self.dsems = []
        self.dvals = []
        self.dnext = 0


class FW:
    def __init__(self, nc, stack, n_dma=14):
        self.nc = nc
        self.engs = {}
        for name in ("tensor", "vector", "scalar", "gpsimd", "sync"):
            sem = stack.enter_context(nc.semaphore("s_" + name))
            self.engs[name] = Eng(name, getattr(nc, name), sem)
        for qn in ("sync", "gpsimd"):
            e = self.engs[qn]
            for i in range(n_dma):
                e.dsems.append(stack.enter_context(nc.semaphore("d_%s%d" % (qn, i))))
                e.dvals.append(0)
        self.n_ops = 0

    def _collect(self, eng, reads, writes, after=()):
        deps = []
        for t in reads:
            if t.w is not None:
                deps.append(t.w)
        for t in list(writes) + list(after):
            if t.w is not None:
                deps.append(t.w)
            deps.extend(t.readers.values())
        waits = {}
        for (sem, val, owner) in deps:
            if owner is eng and not eng.same_wait:
                continue
            k = id(sem)
            if eng.waited.get(k, -1) >= val:
                continue
            if k not in waits or waits[k][1] < val:
                waits[k] = (sem, val)
        for k, (sem, val) in waits.items():
            eng.waited[k] = val
        return list(waits.values())

    def _mark(self, ev, reads, writes):
        k = id(ev[0])
        for t in reads:
            old = t.readers.get(k)
            if old is None or old[1] < ev[1]:
                t.readers[k] = ev
        for t in writes:
            t.w = ev
            t.readers = {}
        self.n_ops += 1

    def op(self, engname, fn, reads=(), writes=(), inc=True, after=()):
        eng = self.engs[engname]
        ex = [t for t in reads if t.excl]
        if ex:
            reads = [t for t in reads if not t.excl]
            writes = list(writes) + ex
        waits = self._collect(eng, reads, writes, after)
        h = eng.handle
        for (s, v) in waits:
            h.wait_ge(s, v)
        if inc:
            eng.count += 1
            ev = (eng.sem, eng.count, eng)
            fn(h).then_inc(eng.sem, 1)
        else:
            ev = (eng.sem, eng.count + 1, eng)
            fn(h)
        self._mark(ev, reads, writes)
        return ev

    def dma(self, qname, fn, reads=(), writes=(), after=()):
        eng = self.engs[qname]
        waits = self._collect(eng, reads, writes, after)
        i = eng.dnext
        eng.dnext = (i + 1) % len(eng.dsems)
        dsem = eng.dsems[i]
        prev = eng.dvals[i]
        if prev > 0 and eng.waited.get(id(dsem), -1) < prev:
            waits.append((dsem, prev))
            eng.waited[id(dsem)] = prev
        eng.dvals[i] = prev + 16
        ev = (dsem, prev + 16, None)
        h = eng.handle
        for (s, v) in waits:
            h.wait_ge(s, v)
        fn(h).then_inc(dsem, 16)
        self._mark(ev, reads, writes)
        return ev

    def barrier(self):
        targets = []
        for e in self.engs.values():
            if e.count > 0:
                targets.append((e.sem, e.count, e))
            for i, ds in enumerate(e.dsems):
                if e.dvals[i] > 0:
                    targets.append((ds, e.dvals[i], None))
        for e in self.engs.values():
            for (sem, val, owner) in targets:
                if owner is e:
                    continue
                if e.waited.get(id(sem), -1) >= val:
                    continue
                e.waited[id(sem)] = val
                e.handle.wait_ge(sem, val)

    def wait_all(self, engname, tiles):
        eng = self.engs[engname]
        waits = self._collect(eng, tiles, tiles)
        for (s, v) in waits:
            eng.handle.wait_ge(s, v)


class Ring:
    def __init__(self, aps, excl=False):
        self.items = [(a, Tile(excl)) for a in aps]
        self.i = 0

    def next(self):
        r = self.items[self.i]
        self.i = (self.i + 1) % len(self.items)
        return r


def _t5_bucket(n):
    nf = np.maximum(n, 1).astype(np.float32)
    large = 16 + (np.log(nf / np.float32(16)) / np.float32(math.log(2048 / 16)) * np.float32(16)).astype(np.int32)
    large = np.minimum(large, 31)
    return np.where(n < 16, n, large)


def _bias_tables(rel_bias):
    k = np.arange(128)[:, None, None]
    j = np.arange(2)[None, :, None]
    q = np.arange(128)[None, None, :]
    dist = q + 128 - (j * 128 + k)
    valid = (dist >= 0) & (dist <= 128)
    out = np.empty((3, 16, 128, 256), np.float32)
    for p, dil in enumerate((1, 4, 16)):
        b = _t5_bucket((np.maximum(dist, 0) * dil).astype(np.int32))
        g = rel_bias[b]
        g = np.where(valid[..., None], g, np.float32(NEG)).astype(np.float32)
        out[p] = np.transpose(g, (3, 0, 1, 2)).reshape(16, 128, 256)
    return out


def _prep_inputs(inp):
    x = np.ascontiguousarray(inp["x"])
    pv = np.concatenate([
        inp["norm1_g"][0].reshape(32, 128), inp["norm2_g"][0].reshape(32, 128),
        inp["final_g"].reshape(32, 128), inp["conv_w"][0].reshape(64, 128),
        inp["conv_b"][0].reshape(16, 128), inp["rg_b_a"][0].reshape(16, 128),
        inp["rg_b_i"][0].reshape(16, 128), inp["lru_lambda"][0].reshape(16, 128),
        inp["gnorm_rec"][0].reshape(16, 128), inp["gnorm_attn"][0].reshape(16, 128),
        inp["ada_b"][0].reshape(192, 128)], axis=0).astype(np.float32)
    assert pv.shape == (R_TOT, 128)
    bt = _bias_tables(inp["rel_bias"])
    bte0 = bt.copy()
    bte0[:, :, :, 0:128] = np.float32(NEG)
    shared = {
        "pvec": pv, "ada_w": inp["ada_w"][0], "w_in": inp["w_in"][0],
        "rg_w_a": inp["rg_w_a"][0], "rg_w_i": inp["rg_w_i"][0], "w_out": inp["w_out"][0],
        "w_gate": inp["w_gate"][0], "w_up": inp["w_up"][0], "w_down": inp["w_down"][0],
        "btab": bt,
    }
    zeros_h = np.zeros((TH, D), np.float32)
    maps = []
    for b in range(4):
        for s in range(2):
            m = dict(shared)
            m["xm"] = x[b, s * T:(s + 1) * T]
            m["xh"] = x[b, 0:TH] if s == 1 else zeros_h
            m["cvec"] = np.ascontiguousarray(inp["c"][b].reshape(128, 32))
            m["flag"] = np.full((128, 2), float(s), np.float32)
            m["btabe"] = bt if s == 1 else bte0
            maps.append(m)
    return maps


def build_program(debug=False, stop_after=None, ffn_passes=2):
    nc = bass.Bass("TRN2", target_bir_lowering=False)

    def din(name, shape, dt=F32):
        return nc.dram_tensor(name, list(shape), dt, kind="ExternalInput").ap()

    def dscr(name, shape, dt):
        kind = "ExternalOutput" if debug else "Internal"
        return nc.dram_tensor(name, list(shape), dt, kind=kind).ap()

    xm = din("xm", [T, D]); xh = din("xh", [TH, D]); cvec = din("cvec", [128, 32])
    flag_d = din("flag", [128, 2]); pvec = din("pvec", [R_TOT, 128])
    ada_w = din("ada_w", [D, 6 * D]); w_in = din("w_in", [D, NCOL])
    rg_w_a = din("rg_w_a", [16, 128, 128]); rg_w_i = din("rg_w_i", [16, 128, 128])
    w_out = din("w_out", [D, D]); w_gate = din("w_gate", [D, DFF]); w_up = din("w_up", [D, DFF])
    w_down = din("w_down", [DFF, D]); btab = din("btab", [3, 16, 128, 256]); btabe = din("btabe", [3, 16, 128, 256])
    y_out = nc.dram_tensor("y", [T, D], F32, kind="ExternalOutput").ap()

    xT_s = dscr("xT_s", [KC, 128, T], F32)
    q_s = dscr("q_s", [16, 128, T], BF16)
    k_s = dscr("k_s", [16, 128, TH + T], BF16)
    v_s = dscr("v_s", [16, 128, TH + T], BF16)
    mix_s = dscr("mix_s", [KC, 128, T], F32)
    mod_dbg = nc.dram_tensor("mod_dbg", [128, 192], F32, kind="ExternalOutput").ap() if debug else None

    with ExitStack() as st:
        fw = FW(nc, st)
        sb = lambda name, shape, dt=F32, stack=st: stack.enter_context(nc.sbuf_tensor(name, list(shape), dt))
        pp = lambda name, shape, dt=F32, stack=st: stack.enter_context(nc.psum_tensor(name, list(shape), dt))

        xT_t = [[Tile() for _ in range(4)] for _ in range(KC)]
        q_t = [[Tile() for _ in range(4)] for _ in range(16)]
        k_t = [[Tile() for _ in range(8)] for _ in range(16)]
        v_t = [[Tile() for _ in range(8)] for _ in range(16)]
        mix_t = [[Tile() for _ in range(4)] for _ in range(KC)]

        ident = sb("ident", [128, 128]); identb = sb("identb", [128, 128], BF16)
        onesb = sb("onesb", [128, 128], BF16)
        t_const = Tile()
        fw.op("vector", lambda h: h.memset(ident[:], 0.0), writes=[t_const])
        fw.op("gpsimd", lambda h: h.affine_select(out=ident[:], in_=ident[:], pattern=[[-1, 128]],
                                                   compare_op=ALU.not_equal, fill=1.0, base=0, channel_multiplier=1),
              reads=[t_const], writes=[t_const])
        fw.op("vector", lambda h: h.tensor_copy(out=identb[:], in_=ident[:]), reads=[t_const], writes=[t_const])
        fw.op("vector", lambda h: h.memset(onesb[:], 1.0), writes=[t_const])

        P_sb = sb("P_sb", [128, R_TOT]); modT = sb("modT", [128, 192])
        W1 = sb("W1", [128, 32]); W2 = sb("W2", [128, 32])
        flag = sb("flag_sb", [128, 2])
        m8sp = sb("m8sp", [128, 16]); m16sp = sb("m16sp", [128, 16])
        t_par = Tile(); t_mod = Tile(); t_mod2 = Tile()
        cs = sb("cs", [128, 32], F32)
        fw.dma("sync", lambda h: h.dma_start(out=flag[:], in_=flag_d), writes=[t_par])

        with ExitStack() as ph:
            ptmp = sb("ptmp", [112, 4, 128], F32, ph)
            crep = sb("crep", [128, 32, 128], BF16, ph)
            wa_ring = Ring([sb("wa%d" % i, [128, 32, 512], BF16, ph) for i in range(2)])
            ps0 = pp("ps0", [128, 512], F32, ph)
            psm = Ring([pp("psm%d" % i, [128, 512], F32, ph) for i in range(2)], excl=True)
            dtmp = Ring([sb("dtmp%d" % i, [128, 4, 128], F32, ph) for i in range(2)])
            t_ptmp = Tile(); t_ps0 = Tile(True); t_cs = Tile()
            fw.dma("sync", lambda h: h.dma_start(out=ptmp[:], in_=pvec.rearrange("(a r) c -> r a c", r=112)), writes=[t_ptmp])
            for a in range(4):
                fw.op("tensor", lambda h, a=a: h.transpose(ps0[:, a * 112:(a + 1) * 112], ptmp[:, a, :], ident[0:112, 0:112]),
                      reads=[t_ptmp, t_const], writes=[t_ps0])
            fw.op("vector", lambda h: h.tensor_copy(out=P_sb[:], in_=ps0[:, 0:R_TOT]), reads=[t_ps0], writes=[t_par])
            fw.dma("sync", lambda h: h.dma_start(out=cs[:], in_=cvec), writes=[t_cs])
            fw.op("scalar", lambda h: h.activation(out=cs[:], in_=cs[:], func=AF.Silu), reads=[t_cs], writes=[t_cs])
            fw.op("vector", lambda h: h.tensor_copy(out=crep[:], in_=cs[:].unsqueeze(2).to_broadcast([128, 32, 128])),
                  reads=[t_cs], writes=[t_cs])
            lam = P_sb[:, R_LAM:R_LAM + 16]
            e_ = sb("sp_e", [128, 16], F32, ph); z_ = sb("sp_z", [128, 16], F32, ph)
            z2 = sb("sp_z2", [128, 16], F32, ph); pz = sb("sp_pz", [128, 16], F32, ph)
            t_sp = Tile()
            fw.op("scalar", lambda h: h.activation(out=e_[:], in_=lam, func=AF.Abs), reads=[t_par], writes=[t_sp])
            fw.op("scalar", lambda h: h.activation(out=e_[:], in_=e_[:], func=AF.Exp, scale=-1.0), reads=[t_sp], writes=[t_sp])
            V = lambda fn, r=(t_sp, t_par), w=(t_sp,): fw.op("vector", fn, reads=list(r), writes=list(w))
            V(lambda h: h.tensor_scalar(out=z_[:], in0=e_[:], scalar1=2.0, scalar2=None, op0=ALU.add))
            V(lambda h: h.reciprocal(out=z_[:], in_=z_[:]))
            V(lambda h: h.tensor_tensor(out=z_[:], in0=z_[:], in1=e_[:], op=ALU.mult))
            V(lambda h: h.tensor_tensor(out=z2[:], in0=z_[:], in1=z_[:], op=ALU.mult))
            V(lambda h: h.tensor_scalar(out=pz[:], in0=z2[:], scalar1=1.0 / 11.0, scalar2=1.0 / 9.0, op0=ALU.mult, op1=ALU.add))
            for cst in (1.0 / 7.0, 1.0 / 5.0, 1.0 / 3.0, 1.0):
                V(lambda h: h.tensor_tensor(out=pz[:], in0=pz[:], in1=z2[:], op=ALU.mult))
                V(lambda h, cst=cst: h.tensor_scalar(out=pz[:], in0=pz[:], scalar1=cst, scalar2=None, op0=ALU.add))
            V(lambda h: h.tensor_tensor(out=pz[:], in0=pz[:], in1=z_[:], op=ALU.mult))
            V(lambda h: h.tensor_scalar(out=z_[:], in0=lam, scalar1=-1.0, scalar2=0.0, op0=ALU.mult, op1=ALU.max))
            V(lambda h: h.scalar_tensor_tensor(out=pz[:], in0=pz[:], scalar=2.0, in1=z_[:], op0=ALU.mult, op1=ALU.add))
            V(lambda h: h.tensor_scalar(out=m8sp[:], in0=pz[:], scalar1=-8.0, scalar2=None, op0=ALU.mult), w=(t_sp, t_par))
            V(lambda h: h.tensor_scalar(out=m16sp[:], in0=pz[:], scalar1=-16.0, scalar2=None, op0=ALU.mult), w=(t_sp, t_par))

            def adaln_chunk(j, crep, t_crep, wa_ring, ps, t_ps, dtmp, t_out):
                wa, t_wa = wa_ring.next()
                for hh in range(2):
                    fw.dma("gpsimd", lambda h, wa=wa, j=j, hh=hh: h.dma_start(
                        out=wa[:, hh * 16:(hh + 1) * 16, :],
                        in_=ada_w[:, j * 512:(j + 1) * 512].rearrange("(p c) n -> p c n", c=32)[:, hh * 16:(hh + 1) * 16, :]),
                        writes=[t_wa])
                for c in range(32):
                    fw.op("tensor", lambda h, ps=ps, wa=wa, c=c: h.matmul(ps[:], lhsT=crep[:, c, :], rhs=wa[:, c, :],
                                                                          start=(c == 0), stop=(c == 31)),
                          reads=[t_crep, t_wa], writes=[t_ps], inc=(c == 31))
                dt_, t_dt = dtmp.next()
                fw.op("vector", lambda h, ps=ps, dt_=dt_: h.tensor_tensor(
                    out=dt_[:], in0=ps[:].rearrange("p (a m) -> p a m", a=4),
                    in1=ident[:].unsqueeze(1).to_broadcast([128, 4, 128]), op=ALU.mult),
                    reads=[t_ps, t_const], writes=[t_dt])
                fw.op("vector", lambda h, dt_=dt_, j=j: h.tensor_reduce(out=modT[:, 4 * j:4 * j + 4], in_=dt_[:], axis=AX.X, op=ALU.add),
                      reads=[t_dt], writes=[t_out])

            for j in range(16):
                ps, t_ps = psm.next()
                adaln_chunk(j, crep, t_cs, wa_ring, ps, t_ps, dtmp, t_mod)
            fw.op("vector", lambda h: h.tensor_tensor(out=modT[:, 0:64], in0=modT[:, 0:64], in1=P_sb[:, R_ADAB:R_ADAB + 64], op=ALU.add),
                  reads=[t_mod, t_par], writes=[t_mod])
            fw.op("vector", lambda h: h.scalar_tensor_tensor(out=W1[:], in0=modT[:, 32:64], scalar=1.0, in1=P_sb[:, R_N1G:R_N1G + 32],
                                                             op0=ALU.add, op1=ALU.mult), reads=[t_mod, t_par], writes=[t_mod])
            fw.barrier()
        t_md = Tile()
        sh1 = modT[:, 0:32]; g1 = modT[:, 64:96]; sh2 = modT[:, 96:128]; g2 = modT[:, 160:192]
        tP = [t_par, t_mod]
        tP2 = [t_par, t_mod, t_mod2]

        if stop_after == 0:
            return _finish(nc, fw, y_out, st, debug, [])

        with ExitStack() as ph:
            hT = sb("hT", [128, KC, 1024], BF16, ph)
            t_hT = [Tile() for _ in range(8)]
            XT = [sb("xt%d" % i, [128, D], F32, ph) for i in range(2)]
            t_XT = [Tile(), Tile()]
            xc = XT[0][:, 0:1024]; r_sb = XT[0][:, 1024:2048]; i_sb = XT[0][:, 2048:3072]; a_sb = XT[0][:, 3072:4096]
            q_sb = XT[1][:, 0:1024]; hs_bufs = [XT[1][:, 1024:2048], XT[1][:, 2048:3072]]
            xcb = sb("xcb", [128, 1024], BF16, ph)
            t_rec = Tile(); t_hsb = [Tile(), Tile()]
            alias0 = [t_rec]; alias1 = [t_rec] + t_hsb
            junk = sb("junk", [128, 2048], BF16, ph); t_junk = Tile()
            st_ring = Ring([sb("stat%d" % i, [128, 8], F32, ph) for i in range(2)])
            rhs2_ring = Ring([sb("rhs2_%d" % i, [128, 256], F32, ph) for i in range(2)])
            for (r2, t_r2) in rhs2_ring.items:
                fw.op("vector", lambda h, r2=r2: h.tensor_copy(out=r2[:, 0:128], in_=ident[:]), reads=[t_const], writes=[t_r2])
            xTst_ring = Ring([sb("xTst%d" % i, [128, 8, 128], F32, ph) for i in range(2)])
            psA = Ring([pp("psA%d" % i, [128, 512], F32, ph) for i in range(2)], excl=True)
            psB = Ring([pp("psB%d" % i, [128, 512], F32, ph) for i in range(4)], excl=True)
            psG = Ring([pp("psG%d" % i, [128, 512], F32, ph) for i in range(2)], excl=True)
            psAx = Ring([])
            psAx.items = psA.items + psB.items
            wt_ring = Ring([sb("wt%d" % i, [128, KC, 256], BF16, ph) for i in range(3)])
            ev_ring = Ring([sb("evb%d" % i, [128, 512], BF16, ph) for i in range(3)])
            wab = sb("wab", [128, 16, 128], BF16, ph); wib = sb("wib", [128, 16, 128], BF16, ph); t_wg = Tile()
            fw.dma("gpsimd", lambda h: h.dma_start(out=wab[:], in_=rg_w_a.rearrange("h i j -> i h j")), writes=[t_wg])
            fw.dma("gpsimd", lambda h: h.dma_start(out=wib[:], in_=rg_w_i.rearrange("h i j -> i h j")), writes=[t_wg])
            xr_ring = Ring([sb("xr%d" % i, [128, 1028], F32, ph) for i in range(4)])
            gl_ring = Ring([sb("gl%d" % i, [128, 1024], F32, ph) for i in range(2)])
            hist = sb("hist", [128, 16, 3], F32, ph); state = sb("state", [128, 16], F32, ph); t_hs = Tile()
            fw.op("vector", lambda h: h.memset(hist[:], 0.0), writes=[t_hs])
            fw.op("vector", lambda h: h.memset(state[:], 0.0), writes=[t_hs])
            cw = lambda tap, hd: P_sb[:, R_CW + tap * 16 + hd:R_CW + tap * 16 + hd + 1]
            hs_i = [0]

            def phase_a(src, tok0, halo):
                def load_x(tt):
                    r0_ = tok0 + tt * 128
                    fw.dma("sync", lambda h: h.dma_start(out=XT[tt % 2][:], in_=src[r0_:r0_ + 128, :]), writes=[t_XT[tt % 2]],
                           after=(alias0 if tt % 2 == 0 else alias1))

                load_x(0)
                for tt in range(8):
                    xt = XT[tt % 2]; t_xt = t_XT[tt % 2]
                    r0 = tok0 + tt * 128
                    if tt + 1 < 8:
                        load_x(tt + 1)
                    stt, t_st = st_ring.next()
                    for hf in range(2):
                        fw.op("scalar", lambda h, xt=xt, stt=stt, hf=hf: h.activation(
                            out=junk[:], in_=xt[:, hf * 2048:(hf + 1) * 2048], func=AF.Square, accum_out=stt[:, 4 + hf:5 + hf]),
                            reads=[t_xt], writes=[t_junk, t_st])
                    fw.op("vector", lambda h, stt=stt: h.tensor_tensor(out=stt[:, 0:1], in0=stt[:, 4:5], in1=stt[:, 5:6], op=ALU.add),
                          reads=[t_st], writes=[t_st])
                    fw.op("vector", lambda h, stt=stt: h.tensor_scalar(out=stt[:, 1:2], in0=stt[:, 0:1], scalar1=1.0 / D, scalar2=EPS,
                                                                       op0=ALU.mult, op1=ALU.add), reads=[t_st], writes=[t_st])
                    fw.op("scalar", lambda h, stt=stt: h.activation(out=stt[:, 2:3], in_=stt[:, 1:2], func=AF.Sqrt), reads=[t_st], writes=[t_st])
                    fw.op("vector", lambda h, stt=stt: h.reciprocal(out=stt[:, 3:4], in_=stt[:, 2:3]), reads=[t_st], writes=[t_st])
                    r2, t_r2 = rhs2_ring.next()
                    fw.op("vector", lambda h, r2=r2, stt=stt: h.tensor_scalar(out=r2[:, 128:256], in0=ident[:], scalar1=stt[:, 3:4], scalar2=None,
                                                                              op0=ALU.mult), reads=[t_st, t_const], writes=[t_r2])
                    for c2 in range(KC // 2):
                        if not halo and c2 % 4 == 0:
                            xTst, t_xTst = xTst_ring.next()
                        ps, t_ps = psAx.next()
                        for u in range(2):
                            c = c2 * 2 + u
                            if halo:
                                fw.op("tensor", lambda h, ps=ps, xt=xt, r2=r2, c=c, u=u: h.matmul(
                                    ps[:, u * 256 + 128:u * 256 + 256], lhsT=xt[:, c * 128:(c + 1) * 128], rhs=r2[:, 128:256], start=True, stop=True),
                                    reads=[t_xt, t_r2], writes=[t_ps], inc=(u == 1))
                            else:
                                fw.op("tensor", lambda h, ps=ps, xt=xt, r2=r2, c=c, u=u: h.matmul(
                                    ps[:, u * 256:(u + 1) * 256], lhsT=xt[:, c * 128:(c + 1) * 128], rhs=r2[:], start=True, stop=True),
                                    reads=[t_xt, t_r2], writes=[t_ps], inc=(u == 1))
                        ev_eng = "scalar" if c2 % 2 == 0 else "vector"
                        if not halo:
                            cq = (c2 % 4) * 2
                            if ev_eng == "scalar":
                                fw.op("scalar", lambda h, ps=ps, xTst=xTst, cq=cq: h.activation(
                                    out=xTst[:, cq:cq + 2, :], in_=ps[:].rearrange("p (u w) -> p u w", u=2)[:, :, 0:128], func=AF.Copy),
                                    reads=[t_ps], writes=[t_xTst])
                            else:
                                fw.op("vector", lambda h, ps=ps, xTst=xTst, cq=cq: h.tensor_copy(
                                    out=xTst[:, cq:cq + 2, :], in_=ps[:].rearrange("p (u w) -> p u w", u=2)[:, :, 0:128]),
                                    reads=[t_ps], writes=[t_xTst])
                        for u in range(2):
                            c = c2 * 2 + u
                            if ev_eng == "scalar":
                                fw.op("scalar", lambda h, ps=ps, c=c, u=u, tt=tt: h.activation(
                                    out=hT[:, c, tt * 128:(tt + 1) * 128], in_=ps[:, u * 256 + 128:u * 256 + 256], func=AF.Identity,
                                    scale=W1[:, c:c + 1], bias=sh1[:, c:c + 1]), reads=[t_ps] + tP, writes=[t_hT[tt]])
                            else:
                                fw.op("vector", lambda h, ps=ps, c=c, u=u, tt=tt: h.tensor_scalar(
                                    out=hT[:, c, tt * 128:(tt + 1) * 128], in0=ps[:, u * 256 + 128:u * 256 + 256],
                                    scalar1=W1[:, c:c + 1], scalar2=sh1[:, c:c + 1], op0=ALU.mult, op1=ALU.add),
                                    reads=[t_ps] + tP, writes=[t_hT[tt]])
                        if not halo and c2 % 4 == 3:
                            blk = r0 // 512
                            cb0 = (c2 // 4) * 8
                            fw.dma("sync", lambda h, xTst=xTst, r0=r0, cb0=cb0: h.dma_start(
                                out=xT_s[cb0:cb0 + 8, :, r0:r0 + 128].rearrange("c p t -> p c t"), in_=xTst[:]),
                                reads=[t_xTst], writes=[xT_t[c][blk] for c in range(cb0, cb0 + 8)])

            def inproj_group(col0, kind, hd0, tok0, halo):
                wt, t_wt = wt_ring.next()
                for hh in range(2):
                    fw.dma("gpsimd", lambda h, wt=wt, hh=hh: h.dma_start(
                        out=wt[:, hh * 16:(hh + 1) * 16, :],
                        in_=w_in[:, col0:col0 + 256].rearrange("(c p) n -> p c n", p=128)[:, hh * 16:(hh + 1) * 16, :]), writes=[t_wt])
                outs = []
                for sub in range(2):
                    hd = hd0 + sub
                    if kind == "xr":
                        xr, t_xr = xr_ring.next()
                        fw.op("vector", lambda h, xr=xr, hd=hd: h.tensor_copy(out=xr[:, 0:3], in_=hist[:, hd, :]), reads=[t_hs], writes=[t_xr])
                    elif kind == "yg":
                        gl, t_gl = gl_ring.next()
                    for blk in range(2):
                        ps, t_ps = psB.next()
                        for c in range(KC):
                            fw.op("tensor", lambda h, ps=ps, wt=wt, c=c, sub=sub, blk=blk: h.matmul(
                                ps[:], lhsT=wt[:, c, sub * 128:(sub + 1) * 128], rhs=hT[:, c, blk * 512:(blk + 1) * 512],
                                start=(c == 0), stop=(c == KC - 1)),
                                reads=[t_wt] + t_hT[blk * 4:blk * 4 + 4], writes=[t_ps], inc=(c == KC - 1))
                        if kind == "xr":
                            if halo:
                                fw.op("scalar", lambda h, ps=ps, xr=xr, blk=blk: h.activation(
                                    out=xr[:, 3 + blk * 512:3 + (blk + 1) * 512], in_=ps[:], func=AF.Copy, scale=flag[:, 0:1]),
                                    reads=[t_ps, t_par], writes=[t_xr])
                            else:
                                fw.op("scalar", lambda h, ps=ps, xr=xr, blk=blk: h.activation(
                                    out=xr[:, 3 + blk * 512:3 + (blk + 1) * 512], in_=ps[:], func=AF.Copy), reads=[t_ps], writes=[t_xr])
                        elif kind == "yg":
                            fw.op("scalar", lambda h, ps=ps, gl=gl, blk=blk: h.activation(
                                out=gl[:, blk * 512:(blk + 1) * 512], in_=ps[:], func=AF.Gelu), reads=[t_ps], writes=[t_gl])
                        else:
                            evb, t_ev = ev_ring.next()
                            if kind == "q":
                                fw.op("scalar", lambda h, ps=ps, evb=evb: h.activation(out=evb[:], in_=ps[:], func=AF.Copy, scale=QSCALE),
                                      reads=[t_ps], writes=[t_ev])
                                g = (tok0 + blk * 512) // 512
                                fw.dma("sync", lambda h, evb=evb, hd=hd, blk=blk: h.dma_start(
                                    out=q_s[hd, :, tok0 + blk * 512:tok0 + (blk + 1) * 512], in_=evb[:]), reads=[t_ev], writes=[q_t[hd][g]])
                            else:
                                fw.op("vector", lambda h, ps=ps, evb=evb: h.tensor_copy(out=evb[:], in_=ps[:]), reads=[t_ps], writes=[t_ev])
                                dst = k_s if kind == "k" else v_s
                                dtl = k_t if kind == "k" else v_t
                                base = (0 if halo else TH) + tok0 + blk * 512
                                fw.dma("sync", lambda h, evb=evb, hd=hd, dst=dst, base=base: h.dma_start(
                                    out=dst[hd, :, base:base + 512], in_=evb[:]), reads=[t_ev], writes=[dtl[hd][base // 512]])
                    if kind == "xr":
                        outs.append((xr, t_xr))
                    elif kind == "yg":
                        outs.append((gl, t_gl))
                return outs

            def rec_conv(hd, xr, t_xr):
                fw.op("vector", lambda h: h.tensor_copy(out=hist[:, hd, :], in_=xr[:, 1024:1027]), reads=[t_xr], writes=[t_hs])
                R_ = [t_xr, t_rec] + tP
                fw.op("vector", lambda h: h.tensor_scalar(out=xc, in0=xr[:, 3:1027], scalar1=cw(3, hd), scalar2=P_sb[:, R_CB + hd:R_CB + hd + 1],
                                                          op0=ALU.mult, op1=ALU.add), reads=R_, writes=[t_rec], after=[t_XT[0]])
                for tap in (2, 1, 0):
                    fw.op("vector", lambda h, tap=tap: h.scalar_tensor_tensor(out=xc, in0=xr[:, tap:tap + 1024], scalar=cw(tap, hd), in1=xc,
                                                                              op0=ALU.mult, op1=ALU.add), reads=R_, writes=[t_rec])
                fw.op("scalar", lambda h: h.activation(out=xcb[:], in_=xc, func=AF.Copy), reads=[t_rec], writes=[t_rec])

            def rec_rest(hd, glp, tok0, halo):
                A1 = [t_XT[0], t_XT[1]]
                for blk in range(2):
                    sl = slice(blk * 512, (blk + 1) * 512)
                    psr, t_psr = psG.next()
                    fw.op("tensor", lambda h, psr=psr, sl=sl: h.matmul(psr[:], lhsT=wab[:, hd, :], rhs=xcb[:, sl], start=True, stop=True),
                          reads=[t_rec, t_wg], writes=[t_psr])
                    psi, t_psi = psG.next()
                    fw.op("tensor", lambda h, psi=psi, sl=sl: h.matmul(psi[:], lhsT=wib[:, hd, :], rhs=xcb[:, sl], start=True, stop=True),
                          reads=[t_rec, t_wg], writes=[t_psi])
                    fw.op("scalar", lambda h, psr=psr, sl=sl: h.activation(out=r_sb[:, sl], in_=psr[:], func=AF.Sigmoid,
                                                                            bias=P_sb[:, R_BA + hd:R_BA + hd + 1]), reads=[t_psr] + tP, writes=[t_rec], after=A1)
                    fw.op("scalar", lambda h, psi=psi, sl=sl: h.activation(out=i_sb[:, sl], in_=psi[:], func=AF.Sigmoid,
                                                                            bias=P_sb[:, R_BI + hd:R_BI + hd + 1]), reads=[t_psi] + tP, writes=[t_rec])
                fw.op("scalar", lambda h: h.activation(out=a_sb, in_=r_sb, func=AF.Exp, scale=m8sp[:, hd:hd + 1]), reads=[t_rec] + tP, writes=[t_rec])
                fw.op("scalar", lambda h: h.activation(out=q_sb, in_=r_sb, func=AF.Exp, scale=m16sp[:, hd:hd + 1]), reads=[t_rec] + tP, writes=[t_rec])
                fw.op("vector", lambda h: h.tensor_tensor(out=i_sb, in0=i_sb, in1=xc, op=ALU.mult), reads=[t_rec], writes=[t_rec])
                fw.op("scalar", lambda h: h.activation(out=q_sb, in_=q_sb, func=AF.Sqrt, scale=-1.0, bias=1.0), reads=[t_rec], writes=[t_rec])
                if halo:
                    fw.op("vector", lambda h: h.scalar_tensor_tensor(out=i_sb, in0=i_sb, scalar=flag[:, 0:1], in1=q_sb,
                                                                     op0=ALU.mult, op1=ALU.mult), reads=[t_rec, t_par], writes=[t_rec])
                else:
                    fw.op("vector", lambda h: h.tensor_tensor(out=i_sb, in0=i_sb, in1=q_sb, op=ALU.mult), reads=[t_rec], writes=[t_rec])
                hi = hs_i[0]; hs_i[0] ^= 1
                hs = hs_bufs[hi]; t_h = t_hsb[hi]
                fw.op("vector", lambda h: h.tensor_tensor_scan(out=hs, data0=a_sb, data1=i_sb, initial=state[:, hd:hd + 1],
                                                               op0=ALU.mult, op1=ALU.add), reads=[t_rec, t_hs], writes=[t_h], after=A1)
                fw.op("vector", lambda h: h.tensor_copy(out=state[:, hd:hd + 1], in_=hs[:, 1023:1024]), reads=[t_h], writes=[t_hs])
                if not halo:
                    gl, t_gl = glp
                    fw.op("vector", lambda h: h.tensor_tensor(out=hs, in0=hs, in1=gl[:], op=ALU.mult), reads=[t_h, t_gl], writes=[t_h])
                    g = tok0 // 512
                    fw.dma("sync", lambda h: h.dma_start(out=mix_s[hd, :, tok0:tok0 + 1024], in_=hs),
                           reads=[t_h], writes=[mix_t[hd][g], mix_t[hd][g + 1]])

            for (src, halo, sti) in ((xh, True, 0), (xh, True, 1), (xm, False, 0), (xm, False, 1))[OPTS.get("st0", 0):OPTS.get("st1", 4)]:
                tok0 = sti * 1024
                phase_a(src, tok0, halo)
                if OPTS.get("only_a"):
                    continue
                pend = None
                for hp in range(9):
                    xrs = inproj_group(hp * 256, "xr", hp * 2, tok0, halo) if hp < 8 else None
                    if pend is not None:
                        php, pxrs = pend
                        rec = not OPTS.get("no_rec")
                        if rec:
                            rec_conv(php * 2, pxrs[0][0], pxrs[0][1])
                        gls = inproj_group(2048 + php * 256, "yg", php * 2, tok0, halo) if not halo else [None, None]
                        if rec:
                            rec_rest(php * 2, gls[0], tok0, halo)
                            rec_conv(php * 2 + 1, pxrs[1][0], pxrs[1][1])
                        if not halo:
                            inproj_group(4096 + php * 256, "q", php * 2, tok0, halo)
                        inproj_group(6144 + php * 256, "k", php * 2, tok0, halo)
                        if rec:
                            rec_rest(php * 2 + 1, gls[1], tok0, halo)
                        inproj_group(8192 + php * 256, "v", php * 2, tok0, halo)
                    pend = (hp, xrs)
            fw.barrier()

        if stop_after == 1:
            fin = [t for l in (q_t, k_t, v_t, mix_t, xT_t) for r in l for t in r]
            return _finish(nc, fw, y_out, st, debug, fin)

        def tok_ap(buf, base, stride):
            if stride == 1:
                return buf[:, base:base + 128]
            res = base % stride
            b0 = base - res
            return buf[:, b0:b0 + 128 * stride].rearrange("p (i r) -> p r i", r=stride)[:, res, :]

        with ExitStack() as ph:
            qh_ring = Ring([sb("qh%d" % i, [128, T], BF16, ph) for i in range(2)])
            kh_ring = Ring([sb("kh%d" % i, [128, TH + T], BF16, ph) for i in range(2)])
            vh_ring = Ring([sb("vh%d" % i, [128, TH + T], BF16, ph) for i in range(2)])
            bt_ring = Ring([sb("bt%d" % i, [128, 2, 3, 256], BF16, ph) for i in range(2)])
            Vt_ring = Ring([sb("Vt%d" % i, [128, 72, 128], BF16, ph) for i in range(2)])
            E_ring = Ring([sb("E%d" % i, [128, 512], BF16, ph) for i in range(4)])
            accO_ring = Ring([sb("accO%d" % i, [128, T], F32, ph) for i in range(2)])
            accD_ring = Ring([sb("accD%d" % i, [128, T], F32, ph) for i in range(2)])
            psS = Ring([pp("psS%d" % i, [128, 512], F32, ph) for i in range(4)], excl=True)
            psO = Ring([pp("psO%d" % i, [128, 512], F32, ph) for i in range(2)], excl=True)
            psD = Ring([pp("psD%d" % i, [128, 512], F32, ph) for i in range(1)], excl=True)
            psV = Ring([pp("psV%d" % i, [128, 512], F32, ph) for i in range(1)], excl=True)

            vslots = []
            for n in range(-1, 16):
                vslots.append((TH + 128 * n, 1))
            for n in range(-1, 4):
                for r in range(4):
                    vslots.append((TH + 512 * n + r, 4))
            for n in range(-1, 1):
                for r in range(16):
                    vslots.append((TH + 2048 * n + r, 16))
            groups = []
            for g in range(4):
                us = []
                for n in range(4 * g, 4 * g + 4):
                    us.append((128 * n, 1, TH + 128 * (n - 1), TH + 128 * n, n, n + 1, n == 0))
                groups.append((0, us, (lambda acc, g=g: acc[:, 512 * g:512 * g + 512].rearrange("p (n i) -> p n i", n=4))))
            for n in range(4):
                us = []
                for r in range(4):
                    us.append((512 * n + r, 4, TH + 512 * (n - 1) + r, TH + 512 * n + r, 17 + n * 4 + r, 17 + (n + 1) * 4 + r, n == 0))
                groups.append((1, us, (lambda acc, n=n: acc[:, 512 * n:512 * n + 512].rearrange("p (i r) -> p r i", r=4))))
            for g in range(4):
                us = []
                for r in range(4 * g, 4 * g + 4):
                    us.append((r, 16, r, TH + r, 37 + r, 37 + 16 + r, True))
                groups.append((2, us, (lambda acc, g=g: acc[:, :].rearrange("p (i r) -> p r i", r=16)[:, 4 * g:4 * g + 4, :])))

            crep2 = sb("crep2", [128, 32, 128], BF16, ph); t_crep2 = Tile()
            fw.op("vector", lambda h: h.tensor_copy(out=crep2[:], in_=cs[:].unsqueeze(2).to_broadcast([128, 32, 128])),
                  reads=[t_cs], writes=[t_crep2])
            wa_ring2 = Ring([sb("wb%d" % i, [128, 32, 512], BF16, ph) for i in range(2)])
            dtmp2 = Ring([sb("dtmq%d" % i, [128, 4, 128], F32, ph) for i in range(2)])

            def load_head(hd):
                qh, t_qh = qh_ring.next(); kh, t_kh = kh_ring.next(); vh, t_vh = vh_ring.next()
                bt, t_bt = bt_ring.next(); Vt, t_Vt = Vt_ring.next()
                fw.dma("sync", lambda h: h.dma_start(out=qh[:], in_=q_s[hd]), reads=q_t[hd], writes=[t_qh])
                fw.dma("sync", lambda h: h.dma_start(out=kh[:], in_=k_s[hd]), reads=k_t[hd], writes=[t_kh])
                fw.dma("sync", lambda h: h.dma_start(out=vh[:], in_=v_s[hd]), reads=v_t[hd], writes=[t_vh])
                fw.dma("gpsimd", lambda h: h.dma_start(out=bt[:, 0, :, :], in_=btab[:, hd].rearrange("p k c -> k p c")), writes=[t_bt])
                fw.dma("gpsimd", lambda h: h.dma_start(out=bt[:, 1, :, :], in_=btabe[:, hd].rearrange("p k c -> k p c")), writes=[t_bt])
                return (qh, t_qh, kh, t_kh, vh, t_vh, bt, t_bt, Vt, t_Vt)

            def vtrans(bufs):
                (qh, t_qh, kh, t_kh, vh, t_vh, bt, t_bt, Vt, t_Vt) = bufs
                for s0 in range(0, 69, 8):
                    n = min(8, 69 - s0)
                    pvf, t_pv = psV.next()
                    pv = pvf[:].bitcast(BF16)
                    for i in range(n):
                        base, stride = vslots[s0 + i]
                        fw.op("tensor", lambda h, i=i, base=base, stride=stride: h.transpose(
                            pv[:, i * 128:(i + 1) * 128], tok_ap(vh, base, stride), identb[:]),
                            reads=[t_vh, t_const], writes=[t_pv], inc=(i == n - 1))
                    if (s0 // 8) % 2 == 0:
                        fw.op("vector", lambda h: h.tensor_copy(
                            out=Vt[:, s0:s0 + n, :], in_=pv[:, 0:n * 128].rearrange("p (a d) -> p a d", d=128)), reads=[t_pv], writes=[t_Vt])
                    else:
                        fw.op("scalar", lambda h: h.activation(
                            out=Vt[:, s0:s0 + n, :], in_=pv[:, 0:n * 128].rearrange("p (a d) -> p a d", d=128), func=AF.Copy), reads=[t_pv], writes=[t_Vt])

            nxt = load_head(0)
            vtrans(nxt)
            for hd in range(16):
                (qh, t_qh, kh, t_kh, vh, t_vh, bt, t_bt, Vt, t_Vt) = nxt
                accO, t_accO = accO_ring.next(); accD, t_accD = accD_ring.next()
                if hd + 1 < 16:
                    nxt = load_head(hd + 1)

                def stage_s(gi):
                    p, us, _ = groups[gi]
                    Es = []
                    for pair in range(2):
                        ps, t_ps = psS.next()
                        for w in range(2):
                            qb, strd, kpb, kcb, vp, vc, edge = us[pair * 2 + w]
                            reg = ps[:, w * 256:(w + 1) * 256]
                            qa = tok_ap(qh, qb, strd)
                            fw.op("tensor", lambda h, reg=reg, e=(1 if edge else 0), p=p: h.matmul(
                                reg, lhsT=identb[:], rhs=bt[:, e, p, :], start=True, stop=False),
                                reads=[t_bt, t_const], writes=[t_ps], inc=False)
                            fw.op("tensor", lambda h, reg=reg, kpb=kpb, strd=strd, qa=qa: h.matmul(
                                reg[:, 0:128], lhsT=tok_ap(kh, kpb, strd), rhs=qa, start=False, stop=False),
                                reads=[t_kh, t_qh], writes=[t_ps], inc=False)
                            fw.op("tensor", lambda h, reg=reg, kcb=kcb, strd=strd, qa=qa: h.matmul(
                                reg[:, 128:256], lhsT=tok_ap(kh, kcb, strd), rhs=qa, start=False, stop=True),
                                reads=[t_kh, t_qh], writes=[t_ps], inc=(w == 1))
                        E, t_E = E_ring.next()
                        fw.op("scalar", lambda h, ps=ps, E=E: h.activation(out=E[:], in_=ps[:], func=AF.Exp), reads=[t_ps], writes=[t_E])
                        Es.append((E, t_E))
                    return Es

                def stage_pv(gi, Es):
                    p, us, accv = groups[gi]
                    po, t_po = psO.next(); pd, t_pd = psD.next()
                    for ui in range(4):
                        qb, strd, kpb, kcb, vp, vc, edge = us[ui]
                        E, t_E = Es[ui // 2]
                        w = ui % 2
                        oreg = po[:, ui * 128:(ui + 1) * 128]
                        dreg = pd[:, ui * 128:(ui + 1) * 128]
                        fw.op("tensor", lambda h, oreg=oreg, vp=vp, E=E, w=w: h.matmul(
                            oreg, lhsT=Vt[:, vp, :], rhs=E[:, w * 256:w * 256 + 128], start=True, stop=False),
                            reads=[t_Vt, t_E], writes=[t_po], inc=False)
                        fw.op("tensor", lambda h, oreg=oreg, vc=vc, E=E, w=w: h.matmul(
                            oreg, lhsT=Vt[:, vc, :], rhs=E[:, w * 256 + 128:w * 256 + 256], start=False, stop=True),
                            reads=[t_Vt, t_E], writes=[t_po], inc=(ui == 3))
                        fw.op("tensor", lambda h, dreg=dreg, E=E, w=w: h.matmul(
                            dreg, lhsT=onesb[:], rhs=E[:, w * 256:w * 256 + 128], start=True, stop=False),
                            reads=[t_const, t_E], writes=[t_pd], inc=False)
                        fw.op("tensor", lambda h, dreg=dreg, E=E, w=w: h.matmul(
                            dreg, lhsT=onesb[:], rhs=E[:, w * 256 + 128:w * 256 + 256], start=False, stop=True),
                            reads=[t_const, t_E], writes=[t_pd], inc=(ui == 3))
                    pov = po[:].rearrange("p (u i) -> p u i", u=4)
                    pdv = pd[:].rearrange("p (u i) -> p u i", u=4)
                    if p == 0:
                        fw.op("vector", lambda h, pov=pov: h.tensor_copy(out=accv(accO), in_=pov), reads=[t_po], writes=[t_accO])
                        fw.op("vector", lambda h, pdv=pdv: h.tensor_copy(out=accv(accD), in_=pdv), reads=[t_pd], writes=[t_accD])
                    else:
                        fw.op("vector", lambda h, pov=pov: h.tensor_tensor(out=accv(accO), in0=pov, in1=accv(accO), op=ALU.add),
                              reads=[t_po, t_accO], writes=[t_accO])
                        fw.op("vector", lambda h, pdv=pdv: h.tensor_tensor(out=accv(accD), in0=pdv, in1=accv(accD), op=ALU.add),
                              reads=[t_pd, t_accD], writes=[t_accD])

                prev = None
                for gi in range(len(groups)):
                    Es = stage_s(gi)
                    if prev is not None:
                        stage_pv(prev[0], prev[1])
                    prev = (gi, Es)
                stage_pv(prev[0], prev[1])
                if hd + 1 < 16:
                    vtrans(nxt)
                for j in (16 + 2 * hd, 17 + 2 * hd):
                    pvf, t_pv = psV.next()
                    adaln_chunk(j, crep2, t_crep2, wa_ring2, pvf, t_pv, dtmp2, t_mod2)
                fw.op("vector", lambda h, accD=accD: h.reciprocal(out=accD[:], in_=accD[:]), reads=[t_accD], writes=[t_accD])
                fw.op("vector", lambda h, accO=accO, accD=accD: h.tensor_tensor(out=accO[:], in0=accO[:], in1=accD[:], op=ALU.mult),
                      reads=[t_accO, t_accD], writes=[t_accO])
                fw.dma("sync", lambda h, accO=accO, hd=hd: h.dma_start(out=mix_s[16 + hd], in_=accO[:]), reads=[t_accO], writes=mix_t[16 + hd])
            fw.op("vector", lambda h: h.tensor_tensor(out=modT[:, 64:192], in0=modT[:, 64:192], in1=P_sb[:, R_ADAB + 64:R_ADAB + 192], op=ALU.add),
                  reads=[t_mod2, t_par], writes=[t_mod2])
            fw.op("vector", lambda h: h.scalar_tensor_tensor(out=W2[:], in0=modT[:, 128:160], scalar=1.0, in1=P_sb[:, R_N2G:R_N2G + 32],
                                                             op0=ALU.add, op1=ALU.mult), reads=[t_mod2, t_par], writes=[t_mod2])
            if debug:
                fw.dma("sync", lambda h: h.dma_start(out=mod_dbg, in_=modT[:]), reads=[t_mod, t_mod2], writes=[t_md])
            fw.barrier()

        if stop_after == 2:
            fin = [t for r in mix_t for t in r]
            return _finish(nc, fw, y_out, st, debug, fin)

        FP = 16
        passes = [(f0, min(FP, NFC - f0)) for f0 in range(0, NFC, FP)]
        y_t = []
        for tb in range(2):
            T0 = tb * 1024
            sb2 = lambda name, shape, dt=F32, stack=None, tb=tb: sb("%s_t%d" % (name, tb), shape, dt, stack)
            pp2 = lambda name, shape, dt=F32, stack=None, tb=tb: pp("%s_t%d" % (name, tb), shape, dt, stack)
            with ExitStack() as ph:
                mh = sb2("mh", [128, KC, 1024], BF16, ph); t_mh = [Tile(), Tile()]
                ld_ring = Ring([sb2("ld%d" % i, [128, 512], F32, ph) for i in range(4)])
                tmp_ring = Ring([sb2("tmp%d" % i, [128, 512], F32, ph) for i in range(2)])
                x1_ring = Ring([sb2("x1_%d" % i, [128, 512], F32, ph) for i in range(2)])
                sq_ring = Ring([sb2("sqb%d" % i, [128, 512], BF16, ph) for i in range(4)])
                rs = [sb2("rs%d" % i, [128, 512], F32, ph) for i in range(2)]; t_rs = [Tile(), Tile()]
                wt_big = sb2("wtbig", [128, 4, KC, 256], BF16, ph)
                wt_ring = Ring([wt_big[:, i] for i in range(4)])
                actT = sb2("actT", [128, FP, 1024], BF16, ph); t_act = [Tile(), Tile()]
                wdn_ring = Ring([sb2("wdn%d" % i, [128, FP, 256], BF16, ph) for i in range(2)])
                psq = Ring([pp2("psq%d" % i, [128, 512], F32, ph) for i in range(2)], excl=True)
                psP = Ring([pp2("psP%d" % i, [128, 512], F32, ph) for i in range(2)], excl=True)
                psGU = Ring([pp2("psGU%d" % i, [128, 512], F32, ph) for i in range(4)], excl=True)
                psPx = Ring([])
                psPx.items = psP.items + psGU.items

                def rstd_from(ps, t_ps, dst, t_dst, n):
                    fw.op("vector", lambda h: h.tensor_scalar(out=dst[:], in0=ps[:], scalar1=1.0 / n, scalar2=EPS, op0=ALU.mult, op1=ALU.add),
                          reads=[t_ps], writes=[t_dst])
                    fw.op("scalar", lambda h: h.activation(out=dst[:], in_=dst[:], func=AF.Sqrt), reads=[t_dst], writes=[t_dst])
                    fw.op("vector", lambda h: h.reciprocal(out=dst[:], in_=dst[:]), reads=[t_dst], writes=[t_dst])

                pend_sq = []

                def emit_ssq(item):
                    (pq, t_pq), sq, t_sq, ncn = item
                    fw.op("tensor", lambda h: h.matmul(pq[:], lhsT=onesb[:], rhs=sq[:], start=(ncn == 0), stop=(ncn == KC - 1)),
                          reads=[t_sq, t_const], writes=[t_pq], inc=True)

                stage = wt_big[:].rearrange("p a c n -> p (a c n)").bitcast(F32).rearrange("p (h c t) -> p h c t", h=2, c=16)
                t_wts4 = [wt_ring.items[i][1] for i in range(4)]
                t_stq = [[Tile() for _ in range(4)] for _ in range(2)]
                stq_all = [t for r in t_stq for t in r]
                it = 0
                for blk in range(2):
                    tk0 = T0 + blk * 512
                    g4 = tk0 // 512
                    for grp in range(2):
                        half = it % 2
                        it += 1
                        for q4 in range(4):
                            c0 = grp * 16 + q4 * 4
                            fw.dma("sync", lambda h, half=half, q4=q4, c0=c0: h.dma_start(
                                out=stage[:, half, q4 * 4:(q4 + 1) * 4, :], in_=mix_s[c0:c0 + 4, :, tk0:tk0 + 512].rearrange("c p t -> p c t")),
                                reads=[mix_t[c0 + i][g4] for i in range(4)], writes=[t_stq[half][q4]], after=t_wts4[2 * half:2 * half + 2])
                        ps, t_ps = psq.next()
                        for c in range(16):
                            sq, t_sq = sq_ring.next()
                            fw.op("scalar", lambda h, sq=sq, half=half, c=c: h.activation(out=sq[:], in_=stage[:, half, c, :], func=AF.Square),
                                  reads=[t_stq[half][c // 4]], writes=[t_sq])
                            fw.op("tensor", lambda h, ps=ps, sq=sq, c=c: h.matmul(ps[:], lhsT=onesb[:], rhs=sq[:], start=(c == 0), stop=(c == 15)),
                                  reads=[t_sq, t_const], writes=[t_ps], inc=True)
                        rstd_from(ps, t_ps, rs[grp], t_rs[grp], 2048.0)
                        for c in range(16):
                            ch = grp * 16 + c
                            gcol = (R_GR if grp == 0 else R_GA) + c
                            fw.op("vector", lambda h, half=half, c=c, ch=ch, gcol=gcol, grp=grp: h.scalar_tensor_tensor(
                                out=mh[:, ch, blk * 512:(blk + 1) * 512], in0=stage[:, half, c, :], scalar=P_sb[:, gcol:gcol + 1], in1=rs[grp][:],
                                op0=ALU.mult, op1=ALU.mult), reads=[t_stq[half][c // 4], t_rs[grp]] + tP2, writes=[t_mh[blk]])

                psn = [psq.next(), psq.next()]
                grp_list = [(nc2 * 2 + sub, blk) for nc2 in range(16) for sub in range(2) for blk in range(2)]
                ld_of = {}

                def issue_ld(i):
                    if i < len(grp_list) and i not in ld_of:
                        ncn_, blk_ = grp_list[i]
                        tk_ = T0 + blk_ * 512
                        ld, t_ld = ld_ring.next()
                        fw.dma("sync", lambda h: h.dma_start(out=ld[:], in_=xT_s[ncn_, :, tk_:tk_ + 512]),
                               reads=[xT_t[ncn_][tk_ // 512]], writes=[t_ld])
                        ld_of[i] = (ld, t_ld)

                issue_ld(0); issue_ld(1)
                gi = 0
                for nc2 in range(16):
                    wt, t_wt = wt_ring.next()
                    for hh in range(2):
                        fw.dma("gpsimd", lambda h, wt=wt, hh=hh, nc2=nc2: h.dma_start(
                            out=wt[:, hh * 16:(hh + 1) * 16, :],
                            in_=w_out[:, nc2 * 256:(nc2 + 1) * 256].rearrange("(c p) n -> p c n", p=128)[:, hh * 16:(hh + 1) * 16, :]), writes=[t_wt],
                            after=(stq_all if nc2 < 4 else ()))
                    for sub in range(2):
                        ncn = nc2 * 2 + sub
                        for blk in range(2):
                            tk0 = T0 + blk * 512
                            g4 = tk0 // 512
                            issue_ld(gi + 2)
                            ps, t_ps = psPx.next()
                            for c in range(KC):
                                fw.op("tensor", lambda h, ps=ps, wt=wt, c=c, sub=sub, blk=blk: h.matmul(
                                    ps[:], lhsT=wt[:, c, sub * 128:(sub + 1) * 128], rhs=mh[:, c, blk * 512:(blk + 1) * 512],
                                    start=(c == 0), stop=(c == KC - 1)), reads=[t_wt, t_mh[blk]], writes=[t_ps], inc=(c == KC - 1))
                            ld, t_ld = ld_of.pop(gi)
                            gi += 1
                            tmp, t_tmp = tmp_ring.next()
                            fw.op("scalar", lambda h, ps=ps, tmp=tmp, ncn=ncn: h.activation(out=tmp[:], in_=ps[:], func=AF.Copy, scale=g1[:, ncn:ncn + 1]),
                                  reads=[t_ps] + tP2, writes=[t_tmp])
                            x1, t_x1 = x1_ring.next()
                            fw.op("vector", lambda h, tmp=tmp, ld=ld, x1=x1: h.tensor_tensor(out=x1[:], in0=tmp[:], in1=ld[:], op=ALU.add),
                                  reads=[t_tmp, t_ld], writes=[t_x1])
                            fw.dma("sync", lambda h, x1=x1, ncn=ncn, tk0=tk0: h.dma_start(out=xT_s[ncn, :, tk0:tk0 + 512], in_=x1[:]),
                                   reads=[t_x1], writes=[xT_t[ncn][g4]])
                            sq, t_sq = sq_ring.next()
                            fw.op("scalar", lambda h, x1=x1, sq=sq: h.activation(out=sq[:], in_=x1[:], func=AF.Square), reads=[t_x1], writes=[t_sq])
                            pend_sq.append((psn[blk], sq, t_sq, ncn))
                            while len(pend_sq) > 2:
                                emit_ssq(pend_sq.pop(0))
                while pend_sq:
                    emit_ssq(pend_sq.pop(0))
                for blk in range(2):
                    rstd_from(psn[blk][0], psn[blk][1], rs[blk], t_rs[blk], float(D))
                for ncn in range(KC):
                    for blk in range(2):
                        tk0 = T0 + blk * 512
                        g4 = tk0 // 512
                        ld, t_ld = ld_ring.next()
                        fw.dma("sync", lambda h, ld=ld, ncn=ncn, tk0=tk0: h.dma_start(out=ld[:], in_=xT_s[ncn, :, tk0:tk0 + 512]),
                               reads=[xT_t[ncn][g4]], writes=[t_ld])
                        tmp, t_tmp = tmp_ring.next()
                        fw.op("vector", lambda h, ld=ld, tmp=tmp, ncn=ncn, blk=blk: h.scalar_tensor_tensor(
                            out=tmp[:], in0=ld[:], scalar=W2[:, ncn:ncn + 1], in1=rs[blk][:], op0=ALU.mult, op1=ALU.mult),
                            reads=[t_ld, t_rs[blk]] + tP2, writes=[t_tmp])
                        fw.op("scalar", lambda h, tmp=tmp, ncn=ncn, blk=blk: h.activation(
                            out=mh[:, ncn, blk * 512:(blk + 1) * 512], in_=tmp[:], func=AF.Identity, bias=sh2[:, ncn:ncn + 1]),
                            reads=[t_tmp] + tP2, writes=[t_mh[blk]])

                if stop_after == 3:
                    fw.barrier()
                    continue

                for pi, (f0, nf) in enumerate(passes):
                    last = pi == len(passes) - 1
                    for fp in range(nf // 2):
                        fcol = (f0 + 2 * fp) * 128
                        wg, t_wg2 = wt_ring.next(); wu, t_wu = wt_ring.next()
                        for (wsrc, wdst, t_w) in ((w_gate, wg, t_wg2), (w_up, wu, t_wu)):
                            for hh in range(2):
                                fw.dma("gpsimd", lambda h, wsrc=wsrc, wdst=wdst, hh=hh, fcol=fcol: h.dma_start(
                                    out=wdst[:, hh * 16:(hh + 1) * 16, :],
                                    in_=wsrc[:, fcol:fcol + 256].rearrange("(c p) n -> p c n", p=128)[:, hh * 16:(hh + 1) * 16, :]), writes=[t_w])
                        for sub in range(2):
                            fl = 2 * fp + sub
                            for blk in range(2):
                                pg, t_pg = psGU.next(); pu, t_pu = psGU.next()
                                for (wsb, t_w, pdst, t_pd) in ((wg, t_wg2, pg, t_pg), (wu, t_wu, pu, t_pu)):
                                    for c in range(KC):
                                        fw.op("tensor", lambda h, pdst=pdst, wsb=wsb, c=c, sub=sub, blk=blk: h.matmul(
                                            pdst[:], lhsT=wsb[:, c, sub * 128:(sub + 1) * 128], rhs=mh[:, c, blk * 512:(blk + 1) * 512],
                                            start=(c == 0), stop=(c == KC - 1)), reads=[t_w, t_mh[blk]], writes=[t_pd], inc=(c == KC - 1))
                                tmp, t_tmp = tmp_ring.next()
                                fw.op("scalar", lambda h, pg=pg, tmp=tmp: h.activation(out=tmp[:], in_=pg[:], func=AF.Silu), reads=[t_pg], writes=[t_tmp])
                                fw.op("vector", lambda h, tmp=tmp, pu=pu, fl=fl, blk=blk: h.tensor_tensor(
                                    out=actT[:, fl, blk * 512:(blk + 1) * 512], in0=tmp[:], in1=pu[:], op=ALU.mult),
                                    reads=[t_tmp, t_pu], writes=[t_act[blk]])
                    if last:
                        psn = [psq.next(), psq.next()]
                    ld_of.clear()
                    issue_ld(0); issue_ld(1)
                    gi = 0
                    for nc2 in range(16):
                        wdn, t_wdn = wdn_ring.next()
                        fw.dma("gpsimd", lambda h, wdn=wdn, nc2=nc2, f0=f0, nf=nf: h.dma_start(
                            out=wdn[:, 0:nf, :],
                            in_=w_down[f0 * 128:(f0 + nf) * 128, nc2 * 256:(nc2 + 1) * 256].rearrange("(c p) n -> p c n", p=128)), writes=[t_wdn])
                        for sub in range(2):
                            ncn = nc2 * 2 + sub
                            for blk in range(2):
                                tk0 = T0 + blk * 512
                                g4 = tk0 // 512
                                issue_ld(gi + 2)
                                ps, t_ps = psPx.next()
                                for c in range(nf):
                                    fw.op("tensor", lambda h, ps=ps, wdn=wdn, c=c, sub=sub, blk=blk: h.matmul(
                                        ps[:], lhsT=wdn[:, c, sub * 128:(sub + 1) * 128], rhs=actT[:, c, blk * 512:(blk + 1) * 512],
                                        start=(c == 0), stop=(c == nf - 1)), reads=[t_wdn, t_act[blk]], writes=[t_ps], inc=(c == nf - 1))
                                ld, t_ld = ld_of.pop(gi)
                                gi += 1
                                tmp, t_tmp = tmp_ring.next()
                                fw.op("scalar", lambda h, ps=ps, tmp=tmp, ncn=ncn: h.activation(out=tmp[:], in_=ps[:], func=AF.Copy, scale=g2[:, ncn:ncn + 1]),
                                      reads=[t_ps] + tP2, writes=[t_tmp])
                                x1, t_x1 = x1_ring.next()
                                fw.op("vector", lambda h, tmp=tmp, ld=ld, x1=x1: h.tensor_tensor(out=x1[:], in0=tmp[:], in1=ld[:], op=ALU.add),
                                      reads=[t_tmp, t_ld], writes=[t_x1])
                                fw.dma("sync", lambda h, x1=x1, ncn=ncn, tk0=tk0: h.dma_start(out=xT_s[ncn, :, tk0:tk0 + 512], in_=x1[:]),
                                       reads=[t_x1], writes=[xT_t[ncn][g4]])
                                if last:
                                    sq, t_sq = sq_ring.next()
                                    fw.op("scalar", lambda h, x1=x1, sq=sq: h.activation(out=sq[:], in_=x1[:], func=AF.Square), reads=[t_x1], writes=[t_sq])
                                    pend_sq.append((psn[blk], sq, t_sq, ncn))
                                    while len(pend_sq) > 2:
                                        emit_ssq(pend_sq.pop(0))
                while pend_sq:
                    emit_ssq(pend_sq.pop(0))
                for blk in range(2):
                    rstd_from(psn[blk][0], psn[blk][1], rs[blk], t_rs[blk], float(D))
                ost_dummy = None
                stg_all = wt_big[:].rearrange("p a c n -> p (a c n)").bitcast(F32)[:, 0:4096].rearrange("p (s j n) -> p s j n", s=8, j=4)
                stg_ring = Ring([stg_all[:, i] for i in range(8)])
                t_wts = [wt_ring.items[i][1] for i in range(4)]
                fin_list = [(blk, ncn) for blk in range(2) for ncn in range(KC)]
                fin_ld = {}

                def issue_fin(i):
                    if i < len(fin_list) and i not in fin_ld:
                        blk_, ncn_ = fin_list[i]
                        tk_ = T0 + blk_ * 512
                        ld, t_ld = ld_ring.next()
                        fw.dma("sync", lambda h: h.dma_start(out=ld[:], in_=xT_s[ncn_, :, tk_:tk_ + 512]),
                               reads=[xT_t[ncn_][tk_ // 512]], writes=[t_ld])
                        fin_ld[i] = (ld, t_ld)

                issue_fin(0); issue_fin(1)
                for fi, (blk, ncn) in enumerate(fin_list):
                    if True:
                        tk0 = T0 + blk * 512
                        g4 = tk0 // 512
                        issue_fin(fi + 2)
                        ld, t_ld = fin_ld.pop(fi)
                        tmp, t_tmp = tmp_ring.next()
                        fw.op("vector", lambda h, ld=ld, tmp=tmp, ncn=ncn, blk=blk: h.scalar_tensor_tensor(
                            out=tmp[:], in0=ld[:], scalar=P_sb[:, R_FG + ncn:R_FG + ncn + 1], in1=rs[blk][:], op0=ALU.mult, op1=ALU.mult),
                            reads=[t_ld, t_rs[blk]] + tP2, writes=[t_tmp])
                        pt, t_pt = psP.next()
                        for j in range(4):
                            fw.op("tensor", lambda h, pt=pt, tmp=tmp, j=j: h.transpose(pt[:, j * 128:(j + 1) * 128], tmp[:, j * 128:(j + 1) * 128], ident[:]),
                                  reads=[t_tmp, t_const], writes=[t_pt], inc=(j == 3))
                        ptv = pt[:].rearrange("p (j n) -> p j n", j=4)
                        stg, t_stg = stg_ring.next()
                        fw.op("scalar", lambda h, ptv=ptv, stg=stg: h.activation(out=stg, in_=ptv, func=AF.Copy),
                              reads=[t_pt], writes=[t_stg], after=t_wts)
                        t_y = Tile()
                        y_t.append(t_y)
                        fw.dma("sync", lambda h, stg=stg, ncn=ncn, tk0=tk0: h.dma_start(
                            out=y_out[tk0:tk0 + 512, ncn * 128:(ncn + 1) * 128].rearrange("(j p) n -> p j n", p=128), in_=stg),
                            reads=[t_stg], writes=[t_y])
                fw.barrier()

        return _finish(nc, fw, y_out, st, debug, y_t + ([t for r in xT_t for t in r] if debug else []))


def _finish(nc, fw, y_out, st, debug, tiles):
    fw.wait_all("sync", tiles)
    return nc


_PROGRAM = None
OPTS = {}


def kernel(**inputs):
    global _PROGRAM
    maps = _prep_inputs(inputs)
    if _PROGRAM is None:
        _PROGRAM = build_program()
    res = run_bass_kernel_spmd(_PROGRAM, maps, core_ids=list(range(8)))
    out = np.empty((4, 4096, D), np.float32)
    for b in range(4):
        for s in range(2):
            out[b, s * T:(s + 1) * T] = res.results[b * 2 + s]["y"]
    return out
```
